# Optimizing a Trainium2 kernel written in Bass

```python
import math
import jax, jax.numpy as jnp
from jax import lax
import numpy as np

D_MODEL = 1024
BATCH = 4
SEQ = 8192
DEPTH = 1

D_MIX = D_MODEL
D_POOL = D_MIX // 2
D_CONV = D_MIX - D_POOL
POOL_WINDOWS = (2, 4, 8, 16)
N_POOL_GROUPS = len(POOL_WINDOWS)
POOL_GROUP_W = D_POOL // N_POOL_GROUPS
CONV_WIDTH = 31
N_CONV_GROUPS = 8
LN_EPS = 1e-5
DEEPNORM_ALPHA = (2.0 * DEPTH) ** 0.25
DEEPNORM_BETA = (8.0 * DEPTH) ** -0.25
IN_SPLITS = (D_POOL, D_POOL, 2 * D_CONV, D_CONV)
D_IN = sum(IN_SPLITS)

kernel_name = "hybrid_pool_conformer_conv_adaln_deepnorm"


def layer_norm(x, eps=LN_EPS):
    x32 = x.astype(jnp.float32)
    mu = jnp.mean(x32, axis=-1, keepdims=True)
    var = jnp.mean(jnp.square(x32 - mu), axis=-1, keepdims=True)
    return ((x32 - mu) * lax.rsqrt(var + eps)).astype(x.dtype)


def causal_multiscale_pool(u):
    b, s, _ = u.shape
    ug = u.reshape(b, s, N_POOL_GROUPS, POOL_GROUP_W).astype(jnp.float32)
    cs = jnp.cumsum(ug, axis=1)
    t = jnp.arange(1, s + 1, dtype=jnp.float32)
    outs = []
    for g, w in enumerate(POOL_WINDOWS):
        cg = cs[:, :, g]
        lagged = jnp.pad(cg[:, :-w], ((0, 0), (w, 0), (0, 0)))
        count = jnp.minimum(t, float(w))[:, None]
        outs.append((cg - lagged) / count)
    pooled = jnp.stack(outs, axis=2)
    return (pooled - ug).astype(u.dtype)


def causal_depthwise_conv(v, w_dw, b_dw):
    y = lax.conv_general_dilated(
        v, w_dw,
        window_strides=(1,),
        padding=[(CONV_WIDTH - 1, 0)],
        dimension_numbers=("NWC", "WIO", "NWC"),
        feature_group_count=v.shape[-1],
    )
    return y + b_dw


def setup_inputs(seed: int = 0) -> dict:
    key = jax.random.key(seed)
    ks = jax.random.split(key, 20)
    f32 = jnp.float32
    nrm = lambda k, shape, s: (jax.random.normal(k, shape, f32) * s)
    inputs = {
        "x": nrm(ks[0], (BATCH, SEQ, D_MODEL), 1.0),
        "c": nrm(ks[1], (BATCH, D_MODEL), 1.0),
        "w_ada": nrm(ks[2], (D_MODEL, 3 * D_MODEL), 0.5 * D_MODEL ** -0.5),
        "b_ada": nrm(ks[3], (3 * D_MODEL,), 0.01),
        "w_in": nrm(ks[4], (D_MODEL, D_IN), D_MODEL ** -0.5),
        "b_in": nrm(ks[5], (D_IN,), 0.01),
        "w_pool": nrm(ks[6], (N_POOL_GROUPS, POOL_GROUP_W, POOL_GROUP_W), DEEPNORM_BETA * POOL_GROUP_W ** -0.5),
        "b_pool": nrm(ks[7], (N_POOL_GROUPS, POOL_GROUP_W), 0.01),
        "ls_pool": 1.0 + nrm(ks[8], (D_POOL,), 0.02),
        "w_dw": nrm(ks[9], (CONV_WIDTH, 1, D_CONV), CONV_WIDTH ** -0.5),
        "b_dw": nrm(ks[10], (D_CONV,), 0.01),
        "ln_conv_g": 1.0 + nrm(ks[11], (D_CONV,), 0.02),
        "ln_conv_b": nrm(ks[12], (D_CONV,), 0.01),
        "w_pw": nrm(ks[13], (D_CONV, D_CONV), DEEPNORM_BETA * D_CONV ** -0.5),
        "b_pw": nrm(ks[14], (D_CONV,), 0.01),
        "w_out": nrm(ks[15], (D_MIX, D_MODEL), DEEPNORM_BETA * D_MIX ** -0.5),
        "b_out": nrm(ks[16], (D_MODEL,), 0.01),
        "ln_post_g": 1.0 + nrm(ks[17], (D_MODEL,), 0.02),
        "ln_post_b": nrm(ks[18], (D_MODEL,), 0.01),
    }
    return inputs


def reference(x, c, w_ada, b_ada, w_in, b_in, w_pool, b_pool, ls_pool, w_dw, b_dw,
              ln_conv_g, ln_conv_b, w_pw, b_pw, w_out, b_out, ln_post_g, ln_post_b):
    for _ in range(DEPTH):
        mod = jax.nn.silu(c) @ w_ada + b_ada
        shift, scale, gate = jnp.split(mod, 3, axis=-1)
        h = layer_norm(x) * (1.0 + scale[:, None, :]) + shift[:, None, :]

        proj = h @ w_in + b_in
        o1 = IN_SPLITS[0]
        o2 = o1 + IN_SPLITS[1]
        o3 = o2 + IN_SPLITS[2]
        u_a, z_a, glu_b, z_b = proj[..., :o1], proj[..., o1:o2], proj[..., o2:o3], proj[..., o3:]

        pooled = causal_multiscale_pool(u_a)
        y_a = jnp.einsum("bsgc,gcd->bsgd", pooled, w_pool) + b_pool
        y_a = y_a.reshape(y_a.shape[0], y_a.shape[1], D_POOL) * ls_pool * jax.nn.silu(z_a)

        v = glu_b[..., :D_CONV] * jax.nn.sigmoid(glu_b[..., D_CONV:])
        v = causal_depthwise_conv(v, w_dw, b_dw)
        v = jax.nn.silu(layer_norm(v) * ln_conv_g + ln_conv_b)
        y_b = (v @ w_pw + b_pw) * jax.nn.silu(z_b)

        y = jnp.concatenate([y_a, y_b], axis=-1) @ w_out + b_out

        x = layer_norm(DEEPNORM_ALPHA * x + gate[:, None, :] * y) * ln_post_g + ln_post_b
    return x
```

```python
import numpy as np
from contextlib import ExitStack
import concourse.bass as bass
import concourse.mybir as mybir
from concourse.bass_utils import run_bass_kernel_spmd

F32 = mybir.dt.float32
BF16 = mybir.dt.bfloat16
AF = mybir.ActivationFunctionType
ALU = mybir.AluOpType
AX = mybir.AxisListType

D = 1024
DIN = 2560
SEQ = 8192
NCORES = 8
TOK = 4096
TT = 512
NSUB = TT // 128
NT = TOK // TT
HL = 32
W = HL + TT
KW = 31
ALPHA = float(2.0 ** 0.25)
EPS = 1e-5
POOLW = (2, 4, 8, 16)
PRIO_TR, PRIO_ST, PRIO_CV = 0.0, 0.0, 0.0
NDVE = 10
NPE = KW - NDVE

C_C, C_BIN, C_BPOOL, C_LS, C_WDW, C_BDW, C_LNG, C_LNB, C_BPW, C_MASK, C_INVC = (
    0, 8, 28, 32, 36, 160, 164, 168, 172, 176, 177)
NCOL = 177 + 64


ALL_BUFS = []


class Buf:
    __slots__ = ("name", "last_w", "readers", "hist", "excl")

    def __init__(self, name, excl=False):
        ALL_BUFS.append(self)
        self.name = name
        self.excl = excl
        self.last_w = None
        self.readers = []
        self.hist = []


class Op:
    __slots__ = ("eng", "fn", "deps", "dma", "stream", "signal", "count",
                 "idx", "eidx", "waits", "lane", "name", "semkey", "cost", "xfer", "alldeps",
                 "t0", "t1", "rank", "clk", "after_issue", "cls", "prio")


class _Probe:
    def __init__(self):
        self.calls = []

    def __getattr__(self, name):
        def f(*a, **kw):
            self.calls.append((name, a, kw))
            return self
        return f


def _fsize(ap):
    n = 1
    for d in tuple(ap.shape)[1:]:
        n *= int(d)
    return n


_ACT_CLS = {}


def _act_class(fn):
    p = _Probe()
    fn(p)
    for name, a, kw in p.calls:
        if name == "activation":
            f = kw.get("func")
            if f == AF.Silu:
                return "silu"
            if f == AF.Sigmoid:
                return "sigmoid"
            if f == AF.Sqrt:
                return "sqrt"
    return None


def _estimate(eng, fn, dma):
    if fn is None:
        return 0.01, 0.0
    p = _Probe()
    fn(p)
    cost, xfer = 0.0, 0.0
    for name, a, kw in p.calls:
        if name == "dma_start":
            src = kw["in_"]
            nbytes = int(src.shape[0]) * _fsize(src) * 4
            cost += 0.1 if eng == "sp" else 0.6
            xfer += 0.3 + nbytes / 330e3
        elif name == "matmul":
            cost += 0.003 + 0.000448 * _fsize(kw["rhs"])
        elif name == "transpose":
            cost += 0.095
        elif name == "activation":
            cost += 0.22 + 0.00095 * _fsize(kw["in_"])
        elif eng == "pool":
            if kw.get("op", None) == ALU.pow:
                cost += 0.5
            else:
                ap = kw.get("out", a[0] if a else None)
                cost += 0.25 + 0.0017 * _fsize(ap)
        else:
            ap = kw.get("in_", kw.get("in0", kw.get("out", a[0] if a else None)))
            n = _fsize(ap)
            if name == "reciprocal":
                cost += 0.1 + 0.0065 * n
            elif name == "bn_aggr":
                cost += 0.15
            else:
                cost += 0.14 + 0.00115 * n
    return cost, xfer


class Prog:
    ENGS = ("pe", "act", "dve", "pool", "sp")

    def __init__(self, nc):
        self.nc = nc
        self.ops = []
        self.per_eng = {e: [] for e in self.ENGS}
        self._uid = 0

    def op(self, eng, fn, reads=(), writes=(), dma=False, stream=None, name="", after_issue=(), after=(),
           prio=0.0):
        o = Op()
        o.prio = prio
        o.after_issue = [a for a in after_issue if a is not None]
        o.eng, o.fn, o.dma, o.name = eng, fn, dma, name
        o.cost, o.xfer = _estimate(eng, fn, dma)
        o.cls = _act_class(fn) if (eng == "act" and fn is not None) else None
        o.deps = {}
        o.signal = False
        o.count = 0
        o.waits = []
        o.idx = len(self.ops)
        o.eidx = len(self.per_eng[eng])
        if dma:
            self._uid += 1
            o.lane = "dma%d" % self._uid
            if stream is None:
                stream = (list(writes) + list(reads))[0].name
            o.stream = stream
        else:
            o.lane = eng
            o.stream = None
        writes = list(writes) + [b for b in reads if b.excl and b not in writes]
        for b in reads:
            if b.last_w is not None:
                o.deps[b.last_w] = True
        for b in writes:
            if b.last_w is not None:
                o.deps.setdefault(b.last_w, False)
            for r in b.readers:
                if r is not o:
                    o.deps.setdefault(r, False)
        for b in reads:
            b.readers.append(o)
            b.hist.append((o, "r"))
        for b in writes:
            b.last_w = o
            b.readers = []
            b.hist.append((o, "w"))
        for a in after:
            o.deps.setdefault(a, False)
        o.deps.pop(o, None)
        self.ops.append(o)
        self.per_eng[eng].append(o)
        return o

    @staticmethod
    def _needs_sync(o, d, raw):
        return True

    def _schedule(self):
        import heapq
        ops = self.ops
        succ = {o: [] for o in ops}
        for o in ops:
            o.alldeps = list(o.deps.keys())
            for d in o.alldeps:
                succ[d].append(o)
        for o in reversed(ops):
            r = 0.0
            for s_ in succ[o]:
                if s_.rank > r:
                    r = s_.rank
            o.rank = r + o.cost + o.xfer + o.prio
        LAT = 0.4
        TBL = 1.3
        act_tbl = [None]
        isucc = {o: [] for o in ops}
        for o in ops:
            for a in o.after_issue:
                assert a.eng == o.eng
                isucc[a].append(o)
        ndeps = {o: len(o.alldeps) + len(o.after_issue) for o in ops}
        ready_t = {o: 0.0 for o in ops}
        ready = {e: [] for e in self.ENGS}
        for o in ops:
            if ndeps[o] == 0:
                heapq.heappush(ready[o.eng], (-o.rank, o.idx, o))
        eng_free = {e: 0.0 for e in self.ENGS}
        dma_free = {e: 0.0 for e in self.ENGS}
        events = []
        t = 0.0
        done = 0
        n = len(ops)
        while done < n:
            progressed = False
            for e in self.ENGS:
                if eng_free[e] > t + 1e-9 or not ready[e]:
                    continue
                cand = [c for c in ready[e] if ready_t[c[2]] <= t + 1e-9]
                if not cand:
                    continue
                extra = 0.0
                if e == "act":
                    same = [c for c in cand if c[2].cls is None or c[2].cls == act_tbl[0]]
                    if same:
                        cand = same
                best = min(cand)
                ready[e].remove(best)
                heapq.heapify(ready[e])
                o = best[2]
                if e == "act" and o.cls is not None and o.cls != act_tbl[0]:
                    act_tbl[0] = o.cls
                    extra = TBL
                o.t0 = t
                eng_free[e] = t + o.cost + extra
                if o.dma:
                    st = max(t + o.cost, dma_free[e])
                    dma_free[e] = st + o.xfer
                    o.t1 = dma_free[e]
                else:
                    o.t1 = t + o.cost + extra
                heapq.heappush(events, (o.t1, o.idx, o))
                for s_ in isucc[o]:
                    ndeps[s_] -= 1
                    if eng_free[e] > ready_t[s_]:
                        ready_t[s_] = eng_free[e]
                    if ndeps[s_] == 0:
                        heapq.heappush(ready[s_.eng], (-s_.rank, s_.idx, s_))
                progressed = True
            if progressed:
                continue
            cands = []
            if events:
                cands.append(events[0][0])
            for e in self.ENGS:
                if ready[e]:
                    tr = min(ready_t[c[2]] for c in ready[e])
                    cands.append(max(tr, eng_free[e]))
            tn = min(cands)
            if tn <= t + 1e-9:
                tn = t + 1e-3
            t = tn
            while events and events[0][0] <= t + 1e-9:
                _, _, o = heapq.heappop(events)
                done += 1
                for s_ in succ[o]:
                    ndeps[s_] -= 1
                    rt = o.t1 + (LAT if (s_.eng != o.eng or o.dma) else 0.0)
                    if rt > ready_t[s_]:
                        ready_t[s_] = rt
                    if ndeps[s_] == 0:
                        heapq.heappush(ready[s_.eng], (-s_.rank, s_.idx, s_))
        self.sim_span = max(o.t1 for o in ops)
        self.ops = sorted(ops, key=lambda o: (o.t0, o.idx))
        self.per_eng = {e: [] for e in self.ENGS}
        for i, o in enumerate(self.ops):
            o.idx = i
            o.eidx = len(self.per_eng[o.eng])
            self.per_eng[o.eng].append(o)

    def _finalize(self):
        self._schedule()
        for o in self.ops:
            o.semkey = ("dma", o.stream) if o.dma else ("eng", o.eng)
        for o in self.ops:
            o.deps = [d for d, raw in o.deps.items() if self._needs_sync(o, d, raw)]
            for d in o.deps:
                d.signal = True
        cnt = {}
        clock = {e: {} for e in self.ENGS}
        snap = {}
        for o in self.ops:
            ck = clock[o.eng]
            waits = {}
            for d in o.deps:
                key = d.semkey
                if ck.get(key, 0) >= d.count:
                    continue
                if waits.get(key, 0) < d.count:
                    waits[key] = d.count
            for key, c in waits.items():
                for k2, c2 in snap[(key, c)].items():
                    if ck.get(k2, 0) < c2:
                        ck[k2] = c2
            o.waits = sorted(waits.items())
            o.clk = dict(ck)
            if o.signal or o.dma:
                key = o.semkey
                cnt[key] = cnt.get(key, 0) + (16 if o.dma else 1)
                o.count = cnt[key]
                s = dict(ck)
                s[key] = o.count
                snap[(key, o.count)] = s
            else:
                o.semkey = None

    def emit(self):
        nc = self.nc
        self._finalize()
        keys = []
        seen = set()
        for o in self.ops:
            if o.semkey is not None and o.semkey not in seen:
                seen.add(o.semkey)
                keys.append(o.semkey)
        self.n_sems = len(keys)
        with ExitStack() as es:
            semh = {}
            for i, k in enumerate(keys):
                semh[k] = es.enter_context(nc.semaphore("s%d" % i))
            block = es.enter_context(nc.Block())

            def run(engname):
                def body(e):
                    for o in self.per_eng[engname]:
                        for key, c in o.waits:
                            e.wait_ge(semh[key], c)
                        ins = o.fn(e) if o.fn is not None else None
                        if o.semkey is not None:
                            if ins is None:
                                raise RuntimeError("signal on empty op " + o.name)
                            ins.then_inc(semh[o.semkey], 16 if o.dma else 1)
                return body

            block.tensor(run("pe"))
            block.scalar(run("act"))
            block.vector(run("dve"))
            block.gpsimd(run("pool"))
            block.sync(run("sp"))


def build_nc():
    del ALL_BUFS[:]
    nc = bass.Bass("TRN2", target_bir_lowering=False)

    def din(name, shape):
        return nc.dram_tensor(name, shape, F32, kind="ExternalInput").ap()

    x = din("x", [HL + TOK, D])
    cols = din("cols", [128, NCOL])
    rows = din("rows", [4, 1024])
    gpb = din("gpb", [128, 2 * D])
    w_ada = din("w_ada", [D, 3 * D])
    w_in = din("w_in", [D, DIN])
    w_pool = din("w_pool", [4, 128, 128])
    w_pw = din("w_pw", [512, 512])
    w_out = din("w_out", [D, D])
    y = nc.dram_tensor("y", [TOK, D], F32, kind="ExternalOutput").ap()
    import os
    DBG = os.environ.get("KDBG") == "1"
    if DBG:
        dbg = nc.dram_tensor("dbg", [128, 4096], F32, kind="ExternalOutput").ap()

    with ExitStack() as es:
        def sb(name, shape, dt):
            return es.enter_context(nc.sbuf_tensor(name, shape, dt))

        def ps(name, shape, dt):
            return es.enter_context(nc.psum_tensor(name, shape, dt))

        P = Prog(nc)

        colt = sb("colt", [128, NCOL], F32); b_colt = Buf("colt")
        bprow = sb("bprow", [1, D], BF16); b_bprow = Buf("bprow")
        gpbt = sb("gpbt", [128, 2 * D], F32); b_gpbt = Buf("gpbt")
        identf = sb("identf", [128, 128], F32); b_identf = Buf("identf")
        ident = sb("ident", [128, 128], BF16); b_ident = Buf("ident")
        onesbf = sb("onesbf", [128, 128], BF16); b_onesbf = Buf("onesbf")
        ones512 = sb("ones512", [128, 128], BF16); b_ones512 = Buf("ones512")
        onesrow = sb("onesrow", [1, 128], BF16); b_onesrow = Buf("onesrow")
        epst = sb("epst", [128, 1], F32); b_epst = Buf("epst")
        nhalf = sb("nhalf", [128, 1], F32); b_nhalf = Buf("nhalf")
        siluc = sb("siluc", [128, 8], F32); b_siluc = Buf("siluc")
        modcol = sb("modcol", [128, 24], F32); b_modcol = Buf("modcol")
        bmask = sb("bmask", [128, 4], F32); b_bmask = Buf("bmask")

        win = sb("win", [128, 8, DIN], BF16); b_win = [Buf("win%d" % i) for i in range(5)]
        wout = sb("wout", [128, 8, D], BF16); b_wout = [Buf("wout%d" % i) for i in range(8)]
        wpw = sb("wpw", [128, 4, 512], BF16); b_wpw = Buf("wpw")
        wpool = sb("wpool", [128, 4, 128], BF16); b_wpool = Buf("wpool")
        diag = sb("diag", [128, 4, NPE, 128], BF16); b_diag = [Buf("diag%d" % i) for i in range(4)]

        xin = [sb("xin%d" % i, [128, D], F32) for i in range(3)]
        b_xin = [Buf("xin%d" % i) for i in range(3)]
        lst = [sb("lst%d" % i, [128, 2, 6], F32) for i in range(3)]
        lmv = [sb("lmv%d" % i, [128, 2], F32) for i in range(3)]
        lve = [sb("lve%d" % i, [128, 1], F32) for i in range(3)]
        lrs = [sb("lrs%d" % i, [128, 1], F32) for i in range(3)]
        lnm = [sb("lnm%d" % i, [128, 1], F32) for i in range(3)]
        b_lst = [Buf("lst%d" % i) for i in range(3)]
        b_lmv = [Buf("lmv%d" % i) for i in range(3)]
        b_lve = [Buf("lve%d" % i) for i in range(3)]
        b_lrs = [Buf("lrs%d" % i) for i in range(3)]
        b_lnm = [Buf("lnm%d" % i) for i in range(3)]
        xn = sb("xn", [128, NSUB, D], BF16); b_xn = [Buf("xn%d" % i) for i in range(NSUB)]
        hTs = [sb("hT%d" % i, [128, 8, TT], BF16) for i in range(2)]
        b_hTs = [[Buf("hT%d_%d" % (i, c)) for c in range(8)] for i in range(2)]
        hTh = sb("hTh", [128, 8, HL], BF16); b_hTh = [Buf("hTh%d" % i) for i in range(8)]
        ubuf = sb("ubuf", [128, 4, W], F32); b_ubuf = [Buf("ubuf%d" % i) for i in range(4)]
        scr = [sb("scr%d" % i, [128, W], F32) for i in range(2)]
        b_scr = [Buf("scr%d" % i) for i in range(2)]
        b_pooled = [Buf("pooled%d" % i) for i in range(4)]
        b_sza = [Buf("sza%d" % i) for i in range(4)]
        b_szb = [Buf("szb%d" % i) for i in range(4)]
        b_sig = [Buf("sig%d" % i) for i in range(4)]
        vbuf = sb("vbuf", [128, 4, W], BF16); b_vbuf = [Buf("vbuf%d" % i) for i in range(4)]
        b_cbf = [Buf("cbf%d" % i) for i in range(4)]
        b_csq = [Buf("csq%d" % i) for i in range(4)]
        msq = sb("msq", [128, TT], F32); b_msq = Buf("msq")
        rstd = sb("rstd", [128, TT], F32); b_rstd = Buf("rstd")
        b_lnv = [Buf("lnv%d" % i) for i in range(4)]
        arena = sb("arena", [128, 10 * 4 * TT], BF16)

        def av(i):
            return arena[:, i * 4 * TT:(i + 1) * 4 * TT].rearrange("p (a b) -> p a b", b=TT)
        sza, szb, sig, pooled, cbf, csq, yb, lnv = [av(i) for i in range(8)]
        ya = [av(8), av(9)]
        b_ya = [[Buf("ya%d_%d" % (i, j)) for j in range(4)] for i in range(2)]
        b_yb = [Buf("yb%d" % i) for i in range(4)]
        NF = 3
        fbuf = [sb("fbuf%d" % i, [128, D], F32) for i in range(NF)]
        b_fbuf = [Buf("fbuf%d" % i) for i in range(NF)]
        fst = [sb("fst%d" % i, [128, 2, 6], F32) for i in range(NF)]
        fmv = [sb("fmv%d" % i, [128, 2], F32) for i in range(NF)]
        fve = [sb("fve%d" % i, [128, 1], F32) for i in range(NF)]
        frs = [sb("frs%d" % i, [128, 1], F32) for i in range(NF)]
        fnm = [sb("fnm%d" % i, [128, 1], F32) for i in range(NF)]
        b_fst = [Buf("fst%d" % i) for i in range(NF)]
        b_fmv = [Buf("fmv%d" % i) for i in range(NF)]
        b_fve = [Buf("fve%d" % i) for i in range(NF)]
        b_frs = [Buf("frs%d" % i) for i in range(NF)]
        b_fnm = [Buf("fnm%d" % i) for i in range(NF)]
        tmp16 = sb("tmp16", [128, 32], F32); b_tmp16 = Buf("tmp16")

        pT = [ps("pT%d" % i, [128, 1024], BF16) for i in range(1)]
        b_pT = [Buf("pT%d" % i, excl=True) for i in range(1)]
        NB = 7
        pF = [ps("pF%d" % i, [128, 512], F32) for i in range(NB)]
        b_pF = [Buf("pF%d" % i, excl=True) for i in range(NB)]
        bank_ctr = [0]

        NQ = 4
        ctr2 = [0, 0]

        def nb(long=False):
            if long:
                i = NQ + ctr2[1] % (NB - NQ)
                ctr2[1] += 1
            else:
                i = ctr2[0] % NQ
                ctr2[0] += 1
            return pF[i], b_pF[i]

        def col(c, n=1):
            return colt[:, c:c + n]

        WA = 3 * D
        wa = [arena[:, 0:WA], arena[:, WA:2 * WA], arena[:, 2 * WA:3 * WA], arena[:, 3 * WA:4 * WA],
              arena[:, 4 * WA:5 * WA],
              hTs[1][:].rearrange("p a b -> p (a b)")[:, 0:WA],
              wout[:].rearrange("p a b -> p (a b)")[:, 0:WA],
              wout[:].rearrange("p a b -> p (a b)")[:, WA:2 * WA]]
        wa_bufs = [b_sza + b_szb[0:2], b_szb[2:4] + b_sig, b_pooled + b_cbf[0:2], b_cbf[2:4] + b_csq,
                   b_yb + b_lnv[0:2], b_hTs[1], b_wout[0:3], b_wout[3:6]]
        yav = [arena[:, (8 + i) * 4 * TT:(9 + i) * 4 * TT] for i in range(2)]
        rowbf_bufs = b_ya[0] + b_ya[1]
        rep = msq[:].bitcast(BF16).rearrange("p (a b) -> p a b", b=128)
        rep_bufs = [b_msq]
        boutst = gpbt
        boutst_bufs = [b_gpbt]

        xin_ctr = [0]
        x_load_ops = []

        def ln_rstd(mv, ve, rs, nm, b_mv, b_ve, b_rs, b_nm, np_, use_pool=True):
            if use_pool:
                P.op("pool", lambda e: e.tensor_scalar(out=ve[0:np_, :], in0=mv[0:np_, 1:2], scalar1=EPS,
                                                       scalar2=None, op0=ALU.add),
                     reads=[b_mv], writes=[b_ve])
                P.op("pool", lambda e: e.tensor_tensor(out=rs[0:np_, :], in0=ve[0:np_, :], in1=nhalf[0:np_, :],
                                                       op=ALU.pow),
                     reads=[b_ve, b_nhalf], writes=[b_rs])
            else:
                P.op("act", lambda e: e.activation(out=ve[0:np_, :], in_=mv[0:np_, 1:2], func=AF.Sqrt,
                                                   bias=epst[0:np_, :], scale=1.0),
                     reads=[b_mv, b_epst], writes=[b_ve])
                P.op("dve", lambda e: e.reciprocal(out=rs[0:np_, :], in_=ve[0:np_, :]),
                     reads=[b_ve], writes=[b_rs])
            P.op("dve", lambda e: e.scalar_tensor_tensor(out=nm[0:np_, :], in0=mv[0:np_, 0:1], scalar=-1.0,
                                                         in1=rs[0:np_, :], op0=ALU.mult, op1=ALU.mult),
                 reads=[b_mv, b_rs], writes=[b_nm])

        xnh = scr[0][:].bitcast(BF16)[0:HL, 0:D]

        def ln1_sub(row0, np_, s, use_pool=True, halo=False, prio=0.0):
            dst = xnh if halo else xn[0:np_, s, :]
            dst_bufs = [b_scr[0]] if halo else [b_xn[s]]
            r = xin_ctr[0] % 3
            xin_ctr[0] += 1
            x_load_ops.append(P.op("sp", lambda e: e.dma_start(out=xin[r][0:np_, :], in_=x[row0:row0 + np_, :]),
                                   writes=[b_xin[r]], dma=True))

            def stats(e):
                e.bn_stats(out=lst[r][0:np_, 0, :], in_=xin[r][0:np_, 0:512])
                return e.bn_stats(out=lst[r][0:np_, 1, :], in_=xin[r][0:np_, 512:1024])
            P.op("dve", stats, reads=[b_xin[r]], writes=[b_lst[r]])
            P.op("dve", lambda e: e.bn_aggr(out=lmv[r][0:np_, :],
                                            in_=lst[r][0:np_].rearrange("p a b -> p (a b)")),
                 reads=[b_lst[r]], writes=[b_lmv[r]])
            ln_rstd(lmv[r], lve[r], lrs[r], lnm[r], b_lmv[r], b_lve[r], b_lrs[r], b_lnm[r], np_, use_pool)
            P.op("act", lambda e: e.activation(out=dst, in_=xin[r][0:np_, :], func=AF.Identity,
                                               bias=lnm[r][0:np_, :], scale=lrs[r][0:np_, :]),
                 reads=[b_xin[r], b_lrs[r], b_lnm[r]], writes=dst_bufs, prio=prio)

        def LN1(k):
            for s in range(NSUB):
                ln1_sub(HL + k * TT + s * 128, 128, s, use_pool=(k >= 2), prio=(60.0 if k == 1 else 0.0))

        def MODT():
            hT, b_hT = hTs[0], b_hTs[0]
            for c in range(8):
                P.op("dve", lambda e, c=c: e.tensor_scalar(out=hT[:, c, :], in0=hT[:, c, :],
                                                          scalar1=modcol[:, 16 + c:17 + c], scalar2=modcol[:, c:c + 1],
                                                          op0=ALU.mult, op1=ALU.add),
                     reads=[b_hT[c], b_modcol], writes=[b_hT[c]])
                P.op("dve", lambda e, c=c: e.tensor_scalar(out=hTh[:, c, :], in0=hTh[:, c, :],
                                                          scalar1=modcol[:, 16 + c:17 + c], scalar2=modcol[:, c:c + 1],
                                                          op0=ALU.mult, op1=ALU.add),
                     reads=[b_hTh[c], b_modcol], writes=[b_hTh[c]])

        def TR(k, raw=False):
            hT, b_hT = hTs[k % 2], b_hTs[k % 2]
            for q in range(4):
                bank, bb = pT[0], b_pT[0]

                def tr(e, q=q, bank=bank):
                    for c in (2 * q, 2 * q + 1):
                        for s in range(NSUB):
                            ins = e.transpose(out=bank[:, (c % 2) * 512 + s * 128:(c % 2) * 512 + (s + 1) * 128],
                                              in_=xn[:, s, c * 128:(c + 1) * 128], identity=ident[:])
                    return ins
                P.op("pe", tr, reads=b_xn + [b_ident], writes=[bb], name="TR%d.%d" % (k, q), prio=PRIO_TR)
                for c in (2 * q, 2 * q + 1):
                    if raw:
                        P.op("act", lambda e, c=c, bank=bank: e.activation(
                            out=hT[:, c, :], in_=bank[:, (c % 2) * 512:(c % 2 + 1) * 512], func=AF.Identity),
                            reads=[bb], writes=[b_hT[c]])
                    else:
                        P.op("act", lambda e, c=c, bank=bank: e.activation(
                            out=hT[:, c, :], in_=bank[:, (c % 2) * 512:(c % 2 + 1) * 512], func=AF.Identity,
                            bias=modcol[:, c:c + 1], scale=modcol[:, 16 + c:17 + c]),
                            reads=[bb, b_modcol], writes=[b_hT[c]])

        P.op("sp", lambda e: e.dma_start(out=colt[:], in_=cols[:, :]), writes=[b_colt], dma=True)
        for q in range(3):
            P.op("sp", lambda e, q=q: e.dma_start(out=fbuf[q][0:1, :], in_=rows[q:q + 1, :]),
                 writes=[b_fbuf[q]], dma=True)
        P.op("sp", lambda e: e.dma_start(out=boutst[0:1, 0:D], in_=rows[3:4, :]), writes=boutst_bufs, dma=True)

        P.op("pool", lambda e: e.memset(identf[:], 0.0), writes=[b_identf])
        P.op("pool", lambda e: e.affine_select(out=identf[:], in_=identf[:], pattern=[[-1, 128]],
                                               compare_op=ALU.not_equal, fill=1.0, base=0,
                                               channel_multiplier=1),
             reads=[b_identf], writes=[b_identf])
        P.op("dve", lambda e: e.tensor_copy(out=ident[:], in_=identf[:]), reads=[b_identf], writes=[b_ident])
        P.op("dve", lambda e: e.memset(onesbf[:], 1.0), writes=[b_onesbf])
        P.op("dve", lambda e: e.memset(ones512[:], 1.0 / 512.0), writes=[b_ones512])
        P.op("dve", lambda e: e.memset(onesrow[:], 1.0), writes=[b_onesrow])
        P.op("dve", lambda e: e.memset(epst[:], EPS), writes=[b_epst])
        P.op("dve", lambda e: e.memset(nhalf[:], -0.5), writes=[b_nhalf])

        ln1_sub(0, HL, 0, use_pool=False, halo=True)
        LN1(0)
        def tr_h(e):
            for c in range(8):
                ins = e.transpose(out=pT[0][:, c * HL:(c + 1) * HL], in_=xnh[:, c * 128:(c + 1) * 128],
                                  identity=ident[0:HL, 0:HL])
            return ins
        P.op("pe", tr_h, reads=[b_scr[0], b_ident], writes=[b_pT[0]])
        for c in range(8):
            P.op("act", lambda e, c=c: e.activation(out=hTh[:, c, :], in_=pT[0][:, c * HL:(c + 1) * HL],
                                                  func=AF.Identity),
                 reads=[b_pT[0]], writes=[b_hTh[c]])
        TR(0, raw=True)

        P.op("act", lambda e: e.activation(out=siluc[:], in_=col(C_C, 8), func=AF.Silu),
             reads=[b_colt], writes=[b_siluc])

        def mk_rep(e):
            for k in range(8):
                ins = e.tensor_scalar(out=rep[:, k, :], in0=onesbf[:], scalar1=siluc[:, k:k + 1],
                                      scalar2=None, op0=ALU.mult)
            return ins
        P.op("dve", mk_rep, reads=[b_onesbf, b_siluc], writes=rep_bufs)

        def mk_rowbf(e):
            for q in range(3):
                ins = e.tensor_copy(out=yav[q // 2][0:1, (q % 2) * 1024:(q % 2 + 1) * 1024], in_=fbuf[q][0:1, :])
            return ins
        P.op("dve", mk_rowbf, reads=b_fbuf, writes=rowbf_bufs)

        tok = []

        def throttle():
            t = Buf("tok%d" % len(tok))
            tok.append(t)
            rd = []
            return rd, [t]
        wa_ops = []
        CA = 2 * D
        for k in range(8):
            trd, twr = throttle()
            wa_ops.append(P.op("pool", lambda e, k=k: e.dma_start(out=wa[k][:, 0:CA],
                                                                in_=w_ada[k * 128:(k + 1) * 128, 0:CA]),
                               reads=trd, writes=wa_bufs[k] + twr, dma=True, stream="waA%d" % k,
                               after_issue=wa_ops[-1:], after=(x_load_ops[0:3] if k < 2 else ())))

            def mm_modA(e, k=k):
                for j in range(4):
                    ins = e.matmul(pF[j][:], lhsT=rep[:, k, :], rhs=wa[k][:, j * 512:(j + 1) * 512],
                                   start=(k == 0), stop=False)
                return ins
            P.op("pe", mm_modA, reads=rep_bufs + wa_bufs[k], writes=b_pF[0:4])

        def mm_modbA(e):
            for j in range(4):
                q = j // 2
                c0 = (q % 2) * 1024 + (j % 2) * 512
                ins = e.matmul(pF[j][:], lhsT=onesbf[0:1, :], rhs=yav[q // 2][0:1, c0:c0 + 512],
                               start=False, stop=True)
            return ins
        P.op("pe", mm_modbA, reads=[b_onesbf] + rowbf_bufs, writes=b_pF[0:4])

        trd, twr = throttle()
        win_first = P.op("pool", lambda e: e.dma_start(
            out=win[:, :, 0:512], in_=w_in[:, 0:512].rearrange("(k p) n -> p k n", p=128)),
            reads=trd, writes=[b_win[0]] + twr, dma=True, after_issue=[wa_ops[7]])

        wb_ops = [win_first]
        for k in range(8):
            trd, twr = throttle()
            wb_ops.append(P.op("pool", lambda e, k=k: e.dma_start(out=wa[k][:, CA:WA],
                                                                in_=w_ada[k * 128:(k + 1) * 128, CA:WA]),
                               reads=trd, writes=wa_bufs[k] + twr, dma=True, stream="waB%d" % k,
                               after_issue=wb_ops[-1:]))

            def mm_modB(e, k=k):
                for j in (4, 5):
                    ins = e.matmul(pF[j][:], lhsT=rep[:, k, :], rhs=wa[k][:, j * 512:(j + 1) * 512],
                                   start=(k == 0), stop=False)
                return ins
            P.op("pe", mm_modB, reads=rep_bufs + wa_bufs[k], writes=b_pF[4:6])

        def mm_modbB(e):
            for j in (4, 5):
                q = j // 2
                c0 = (q % 2) * 1024 + (j % 2) * 512
                ins = e.matmul(pF[j][:], lhsT=onesbf[0:1, :], rhs=yav[q // 2][0:1, c0:c0 + 512],
                               start=False, stop=True)
            return ins
        P.op("pe", mm_modbB, reads=[b_onesbf] + rowbf_bufs, writes=b_pF[4:6])
        gate_sb = [msq, rstd]
        b_gate_sb = [b_msq, b_rstd]
        for h in range(2):
            P.op("dve", lambda e, h=h: e.tensor_copy(out=gate_sb[h][:], in_=pF[4 + h][:]),
                 reads=[b_pF[4 + h]], writes=[b_gate_sb[h]])
        bank_ctr[0] = 0

        prev = wb_ops[-1]
        for g in (1, 3, 2, 4):
            trd, twr = throttle()
            prev = P.op("pool", lambda e, g=g: e.dma_start(
                out=win[:, :, g * 512:(g + 1) * 512],
                in_=w_in[:, g * 512:(g + 1) * 512].rearrange("(k p) n -> p k n", p=128)),
                reads=trd, writes=[b_win[g]] + twr, dma=True, after_issue=[prev])
        for j in range(4):
            def mk_diag(e, j=j):
                for t in range(NPE):
                    ins = e.tensor_scalar(out=diag[:, j, t, :], in0=ident[:],
                                          scalar1=col(C_WDW + j * KW + t), scalar2=None, op0=ALU.mult)
                return ins
            P.op("dve", mk_diag, reads=[b_ident, b_colt], writes=[b_diag[j]])
        P.op("dve", lambda e: e.tensor_scalar(out=bmask[:], in0=col(C_BIN, 4), scalar1=col(C_MASK),
                                              scalar2=None, op0=ALU.mult),
             reads=[b_colt], writes=[b_bmask])

        fb0v = fbuf[0][:].rearrange("p (a b) -> p a b", b=128)
        fb1v = fbuf[1][:].rearrange("p (a b) -> p a b", b=128)
        for half, fbv, bfb in ((0, fb0v, b_fbuf[0]), (1, fb1v, b_fbuf[1])):
            def ext(e, half=half, fbv=fbv):
                for jb in range(8):
                    bank = pF[2 * half + jb // 4]
                    ins = e.tensor_tensor(out=fbv[:, jb, :], in0=bank[:, (jb % 4) * 128:(jb % 4 + 1) * 128],
                                          in1=identf[:], op=ALU.mult)
                return ins
            P.op("dve", ext, reads=[b_pF[2 * half], b_pF[2 * half + 1], b_identf], writes=[bfb])
            P.op("dve", lambda e, half=half, fbv=fbv: e.tensor_reduce(
                out=modcol[:, 8 * half:8 * half + 8], in_=fbv, axis=AX.X, op=ALU.add),
                reads=[bfb], writes=[b_modcol])
        P.op("dve", lambda e: e.tensor_scalar(out=modcol[:, 16:24], in0=modcol[:, 8:16], scalar1=1.0,
                                              scalar2=None, op0=ALU.add),
             reads=[b_modcol], writes=[b_modcol])
        def mk_bprow(e):
            e.tensor_tensor(out=bprow[0:1, 0:512], in0=gate_sb[0][0:1, :], in1=boutst[0:1, 0:512], op=ALU.mult)
            return e.tensor_tensor(out=bprow[0:1, 512:1024], in0=gate_sb[1][0:1, :], in1=boutst[0:1, 512:1024],
                                   op=ALU.mult)
        P.op("dve", mk_bprow, reads=b_gate_sb + boutst_bufs, writes=[b_bprow])

        prev = P.op("pool", lambda e: e.dma_start(out=wpool[:], in_=w_pool.rearrange("g c d -> c g d")),
                    reads=[b_modcol], writes=[b_wpool], dma=True, after_issue=[prev])
        for k in range(8):
            prev = P.op("pool", lambda e, k=k: e.dma_start(out=wout[:, k, :], in_=w_out[k * 128:(k + 1) * 128, :]),
                        reads=[b_modcol], writes=[b_wout[k]], dma=True, after_issue=[prev])
        prev = P.op("pool", lambda e: e.dma_start(out=wpw[:], in_=w_pw.rearrange("(k p) n -> p k n", p=128)),
                    reads=[b_modcol], writes=[b_wpw], dma=True, after_issue=[prev])
        prev = P.op("pool", lambda e: e.dma_start(out=gpbt[:], in_=gpb[:, :]), reads=[b_modcol], writes=[b_gpbt],
                    dma=True, after_issue=[prev], stream="gpbt_sw")

        for k in range(8):
            def fold(e, k=k):
                for h in range(2):
                    if k < 4:
                        ins = e.scalar_tensor_tensor(out=wout[:, k, h * 512:(h + 1) * 512],
                                                     in0=wout[:, k, h * 512:(h + 1) * 512],
                                                     scalar=col(C_LS + k), in1=gate_sb[h][:],
                                                     op0=ALU.mult, op1=ALU.mult)
                    else:
                        ins = e.tensor_tensor(out=wout[:, k, h * 512:(h + 1) * 512],
                                              in0=wout[:, k, h * 512:(h + 1) * 512],
                                              in1=gate_sb[h][:], op=ALU.mult)
                return ins
            P.op("dve", fold, reads=[b_wout[k], b_colt] + b_gate_sb, writes=[b_wout[k]])

        def ip_group(m, n0, n, src=None, src_bufs=None):
            bank, bb = nb()

            def mm(e):
                for kc in range(8):
                    ins = e.matmul(bank[:, 0:n], lhsT=win[:, kc, m * 128:(m + 1) * 128],
                                   rhs=src[:, kc, n0:n0 + n], start=(kc == 0), stop=(kc == 7))
                return ins
            P.op("pe", mm, reads=src_bufs + [b_win[m // 4]], writes=[bb], name="IP.m%d.n%d" % (m, n))
            return bank, bb

        def halo_copy(buf, bufs):
            P.op("pool", lambda e: e.tensor_copy(out=buf[:, :, 0:HL], in_=buf[:, :, TT:TT + HL]),
                 reads=bufs, writes=bufs)

        def IP_u(k):
            if k > 0:
                halo_copy(ubuf, b_ubuf)
            for m in range(4):
                bank, bb = ip_group(m, 0, TT, hTs[k % 2], b_hTs[k % 2])
                P.op("act", lambda e, m=m, bank=bank: e.activation(
                    out=ubuf[:, m, HL:W], in_=bank[:], func=AF.Identity, bias=col(C_BIN + m), scale=1.0),
                    reads=[bb, b_colt], writes=[b_ubuf[m]])

        def IP_za(k):
            for j in range(4):
                m = 4 + j
                bank, bb = ip_group(m, 0, TT, hTs[k % 2], b_hTs[k % 2])
                P.op("act", lambda e, m=m, j=j, bank=bank: e.activation(
                    out=sza[:, j, :], in_=bank[:], func=AF.Silu, bias=col(C_BIN + m), scale=1.0),
                    reads=[bb, b_colt], writes=[b_sza[j]])

        def IP_zb(k):
            for j in range(4):
                m = 16 + j
                bank, bb = ip_group(m, 0, TT, hTs[k % 2], b_hTs[k % 2])
                P.op("act", lambda e, m=m, j=j, bank=bank: e.activation(
                    out=szb[:, j, :], in_=bank[:], func=AF.Silu, bias=col(C_BIN + m), scale=1.0),
                    reads=[bb, b_colt], writes=[b_szb[j]])

        def IP_glu(k):
            if k > 0:
                halo_copy(vbuf, b_vbuf)
            for j in range(4):
                m = 12 + j
                bank, bb = ip_group(m, 0, TT, hTs[k % 2], b_hTs[k % 2])
                P.op("act", lambda e, m=m, j=j, bank=bank: e.activation(
                    out=sig[:, j, :], in_=bank[:], func=AF.Sigmoid, bias=col(C_BIN + m), scale=1.0),
                    reads=[bb, b_colt], writes=[b_sig[j]])
            for j in range(4):
                m = 8 + j
                bank, bb = ip_group(m, 0, TT, hTs[k % 2], b_hTs[k % 2])
                P.op("dve", lambda e, m=m, j=j, bank=bank: e.scalar_tensor_tensor(
                    out=vbuf[:, j, HL:W], in0=bank[:], scalar=col(C_BIN + m), in1=sig[:, j, :],
                    op0=ALU.add, op1=ALU.mult),
                    reads=[bb, b_colt, b_sig[j]], writes=[b_vbuf[j]])

        def POOLING(k):
            for g, w in enumerate(POOLW):
                A, bA = scr[0], b_scr[0]
                Bt, bB = scr[1], b_scr[1]
                src = ubuf[:, g, :]
                P.op("pool", lambda e, A=A, src=src: e.tensor_tensor(
                    out=A[:, 1:W], in0=src[:, 1:W], in1=src[:, 0:W - 1], op=ALU.add),
                    reads=[b_ubuf[g]], writes=[bA])
                S, bS = A, bA
                if w >= 4:
                    P.op("pool", lambda e, A=A, Bt=Bt: e.tensor_tensor(
                        out=Bt[:, 3:W], in0=A[:, 3:W], in1=A[:, 1:W - 2], op=ALU.add),
                        reads=[bA], writes=[bB])
                    S, bS = Bt, bB
                if w >= 8:
                    P.op("pool", lambda e, A=A, Bt=Bt: e.tensor_tensor(
                        out=A[:, 7:W], in0=Bt[:, 7:W], in1=Bt[:, 3:W - 4], op=ALU.add),
                        reads=[bB], writes=[bA])
                    S, bS = A, bA
                if w >= 16:
                    P.op("pool", lambda e, A=A, Bt=Bt: e.tensor_tensor(
                        out=Bt[:, 15:W], in0=A[:, 15:W], in1=A[:, 7:W - 8], op=ALU.add),
                        reads=[bA], writes=[bB])
                    S, bS = Bt, bB
                P.op("dve", lambda e, S=S, g=g, w=w, src=src: e.scalar_tensor_tensor(
                    out=pooled[:, g, :], in0=S[:, HL:W], scalar=1.0 / w, in1=src[:, HL:W],
                    op0=ALU.mult, op1=ALU.subtract),
                    reads=[bS, b_ubuf[g]], writes=[b_pooled[g]])
                if k == 0:
                    P.op("dve", lambda e, S=S, g=g: e.tensor_tensor(
                        out=tmp16[:, 0:16], in0=S[:, HL:HL + 16],
                        in1=colt[:, C_INVC + 16 * g:C_INVC + 16 * g + 16], op=ALU.mult),
                        reads=[bS, b_colt], writes=[b_tmp16])
                    P.op("dve", lambda e, g=g, src=src: e.tensor_tensor(
                        out=pooled[:, g, 0:16], in0=tmp16[:, 0:16], in1=src[:, HL:HL + 16], op=ALU.subtract),
                        reads=[b_tmp16, b_ubuf[g]], writes=[b_pooled[g]])

        def GL(k):
            par = k % 2
            for g in range(4):
                bank, bb = nb()
                P.op("pe", lambda e, g=g, bank=bank: e.matmul(bank[:], lhsT=wpool[:, g, :], rhs=pooled[:, g, :],
                                                             start=True, stop=True),
                     reads=[b_wpool, b_pooled[g]], writes=[bb], name="GL%d" % k)
                P.op("dve", lambda e, g=g, bank=bank: e.scalar_tensor_tensor(
                    out=ya[par][:, g, :], in0=bank[:], scalar=col(C_BPOOL + g), in1=sza[:, g, :],
                    op0=ALU.add, op1=ALU.mult),
                    reads=[bb, b_colt, b_sza[g]], writes=[b_ya[par][g]])


        def CV(k):
            npe = NPE
            for j in range(4):
                bank, bb = nb(True)

                def mm(e, j=j, bank=bank):
                    for t in range(npe):
                        ins = e.matmul(bank[:], lhsT=diag[:, j, t, :],
                                       rhs=vbuf[:, j, HL - (KW - 1) + t:HL - (KW - 1) + t + TT],
                                       start=(t == 0), stop=(t == npe - 1))
                    return ins
                P.op("pe", mm, reads=[b_diag[j], b_vbuf[j]], writes=[bb], name="CV%d.%d" % (k, j), prio=PRIO_CV)
                for t in range(npe, KW):
                    P.op("dve", lambda e, j=j, t=t, bank=bank: e.scalar_tensor_tensor(
                        out=bank[:], in0=vbuf[:, j, HL - (KW - 1) + t:HL - (KW - 1) + t + TT],
                        scalar=col(C_WDW + j * KW + t), in1=bank[:], op0=ALU.mult, op1=ALU.add),
                        reads=[bb, b_vbuf[j], b_colt], writes=[bb])
                P.op("act", lambda e, j=j, bank=bank: e.activation(
                    out=cbf[:, j, :], in_=bank[:], func=AF.Identity, bias=col(C_BDW + j), scale=1.0),
                    reads=[bb, b_colt], writes=[b_cbf[j]])
                P.op("act", lambda e, j=j, bank=bank: e.activation(
                    out=csq[:, j, :], in_=bank[:], func=AF.Square, bias=col(C_BDW + j), scale=1.0),
                    reads=[bb, b_colt], writes=[b_csq[j]])

        def ST(k):
            bM, bbM = nb(True)
            bQ, bbQ = nb(True)

            def mmM(e):
                for j in range(4):
                    ins = e.matmul(bM[:], lhsT=ones512[:], rhs=cbf[:, j, :], start=(j == 0), stop=(j == 3))
                return ins
            P.op("pe", mmM, reads=[b_ones512] + b_cbf, writes=[bbM], name="STm%d" % k, prio=PRIO_ST)

            def mmQ(e):
                for j in range(4):
                    ins = e.matmul(bQ[:], lhsT=ones512[:], rhs=csq[:, j, :], start=(j == 0), stop=(j == 3))
                return ins
            P.op("pe", mmQ, reads=[b_ones512] + b_csq, writes=[bbQ], name="STq%d" % k, prio=PRIO_ST)
            P.op("act", lambda e: e.activation(out=msq[:], in_=bM[:], func=AF.Square),
                 reads=[bbM], writes=[b_msq])
            P.op("dve", lambda e: e.tensor_tensor(out=msq[:], in0=bQ[:], in1=msq[:], op=ALU.subtract),
                 reads=[bbQ, b_msq], writes=[b_msq])
            P.op("act", lambda e: e.activation(out=msq[:], in_=msq[:], func=AF.Sqrt, bias=epst[:], scale=1.0),
                 reads=[b_msq, b_epst], writes=[b_msq])
            P.op("dve", lambda e: e.reciprocal(out=rstd[:], in_=msq[:]), reads=[b_msq], writes=[b_rstd])
            for j in range(4):
                P.op("dve", lambda e, j=j: e.tensor_tensor(out=cbf[:, j, :], in0=cbf[:, j, :], in1=bM[:],
                                                         op=ALU.subtract),
                     reads=[b_cbf[j], bbM], writes=[b_cbf[j]])
                P.op("pool", lambda e, j=j: e.tensor_tensor(out=cbf[:, j, :], in0=cbf[:, j, :], in1=rstd[:],
                                                          op=ALU.mult),
                     reads=[b_cbf[j], b_rstd], writes=[b_cbf[j]])
                P.op("act", lambda e, j=j: e.activation(out=lnv[:, j, :], in_=cbf[:, j, :], func=AF.Silu,
                                                      bias=col(C_LNB + j), scale=col(C_LNG + j)),
                     reads=[b_cbf[j], b_colt], writes=[b_lnv[j]])

        def PW(k):
            for mo in range(4):
                bank, bb = nb()

                def mm(e, mo=mo, bank=bank):
                    for ki in range(4):
                        ins = e.matmul(bank[:], lhsT=wpw[:, ki, mo * 128:(mo + 1) * 128], rhs=lnv[:, ki, :],
                                       start=(ki == 0), stop=(ki == 3))
                    return ins
                P.op("pe", mm, reads=[b_wpw] + b_lnv, writes=[bb], name="PW%d.%d" % (k, mo))
                P.op("dve", lambda e, mo=mo, bank=bank: e.scalar_tensor_tensor(
                    out=yb[:, mo, :], in0=bank[:], scalar=col(C_BPW + mo), in1=szb[:, mo, :],
                    op0=ALU.add, op1=ALU.mult),
                    reads=[bb, b_colt, b_szb[mo]], writes=[b_yb[mo]])

        f_ctr = [0]
        stores = []

        def OPF(k):
            par = k % 2
            for s in range(NSUB):
                r = f_ctr[0] % NF
                f_ctr[0] += 1
                row0 = k * TT + s * 128
                P.op("sp", lambda e, r=r, row0=row0: e.dma_start(out=fbuf[r][:], in_=x[HL + row0:HL + row0 + 128, :]),
                     writes=[b_fbuf[r]], dma=True)
                for h in range(2):
                    bank, bb = nb()

                    def mm(e, h=h, bank=bank, s=s):
                        for kc in range(8):
                            lhs = ya[par][:, kc, s * 128:(s + 1) * 128] if kc < 4 else yb[:, kc - 4, s * 128:(s + 1) * 128]
                            e.matmul(bank[:], lhsT=lhs, rhs=wout[:, kc, h * 512:(h + 1) * 512],
                                     start=(kc == 0), stop=False)
                        return e.matmul(bank[:], lhsT=onesrow[0:1, :], rhs=bprow[0:1, h * 512:(h + 1) * 512],
                                        start=False, stop=True)
                    P.op("pe", mm, reads=b_ya[par] + b_yb + b_wout + [b_onesrow, b_bprow], writes=[bb],
                         name="OP%d.%d.%d" % (k, s, h))

                    P.op("dve", lambda e, h=h, bank=bank, r=r: e.scalar_tensor_tensor(
                        out=fbuf[r][:, h * 512:(h + 1) * 512], in0=fbuf[r][:, h * 512:(h + 1) * 512],
                        scalar=ALPHA, in1=bank[:], op0=ALU.mult, op1=ALU.add),
                        reads=[b_fbuf[r], bb], writes=[b_fbuf[r]])
                    P.op("dve", lambda e, h=h, r=r: e.bn_stats(out=fst[r][:, h, :],
                                                               in_=fbuf[r][:, h * 512:(h + 1) * 512]),
                         reads=[b_fbuf[r]], writes=[b_fst[r]])
                P.op("dve", lambda e, r=r: e.bn_aggr(out=fmv[r][:], in_=fst[r][:].rearrange("p a b -> p (a b)")),
                     reads=[b_fst[r]], writes=[b_fmv[r]])
                ln_rstd(fmv[r], fve[r], frs[r], fnm[r], b_fmv[r], b_fve[r], b_frs[r], b_fnm[r], 128)
                P.op("act", lambda e, r=r: e.activation(out=fbuf[r][:], in_=fbuf[r][:], func=AF.Identity,
                                                      bias=fnm[r][:], scale=frs[r][:]),
                     reads=[b_fbuf[r], b_frs[r], b_fnm[r]], writes=[b_fbuf[r]])
                aff_eng = "dve" if k == NT - 1 else "pool"
                P.op(aff_eng, lambda e, r=r: e.tensor_tensor(out=fbuf[r][:], in0=fbuf[r][:], in1=gpbt[:, 0:D],
                                                           op=ALU.mult),
                     reads=[b_fbuf[r], b_gpbt], writes=[b_fbuf[r]])
                P.op(aff_eng, lambda e, r=r: e.tensor_tensor(out=fbuf[r][:], in0=fbuf[r][:], in1=gpbt[:, D:2 * D],
                                                           op=ALU.add),
                     reads=[b_fbuf[r], b_gpbt], writes=[b_fbuf[r]])
                P.op("sp", lambda e, r=r, row0=row0: e.dma_start(out=y[row0:row0 + 128, :], in_=fbuf[r][:]),
                     reads=[b_fbuf[r]], dma=True)


        MODT()
        for m in range(4):
            bank, bb = ip_group(m, 0, HL, hTh, b_hTh)
            P.op("act", lambda e, m=m, bank=bank: e.activation(
                out=ubuf[:, m, 0:HL], in_=bank[:, 0:HL], func=AF.Identity, bias=bmask[:, m:m + 1],
                scale=col(C_MASK)),
                reads=[bb, b_colt, b_bmask], writes=[b_ubuf[m]])
        for j in range(4):
            m = 12 + j
            bank, bb = ip_group(m, 0, HL, hTh, b_hTh)
            P.op("act", lambda e, m=m, j=j, bank=bank: e.activation(
                out=sig[:, j, 0:HL], in_=bank[:, 0:HL], func=AF.Sigmoid, bias=col(C_BIN + m), scale=1.0),
                reads=[bb, b_colt], writes=[b_sig[j]])
        for j in range(4):
            m = 8 + j
            bank, bb = ip_group(m, 0, HL, hTh, b_hTh)
            P.op("dve", lambda e, m=m, j=j, bank=bank: e.scalar_tensor_tensor(
                out=tmp16[:, 0:HL], in0=bank[:, 0:HL], scalar=col(C_BIN + m), in1=sig[:, j, 0:HL],
                op0=ALU.add, op1=ALU.mult),
                reads=[bb, b_colt, b_sig[j]], writes=[b_tmp16])
            P.op("dve", lambda e, j=j: e.tensor_scalar(out=vbuf[:, j, 0:HL], in0=tmp16[:, 0:HL],
                                                     scalar1=col(C_MASK), scalar2=None, op0=ALU.mult),
                 reads=[b_tmp16, b_colt], writes=[b_vbuf[j]])

        for k in range(NT + 1):
            front = k < NT
            back = k >= 1
            if back:
                CV(k - 1)
            if front:
                if k > 0:
                    TR(k)
                IP_u(k)
                if k + 1 < NT:
                    LN1(k + 1)
                POOLING(k)
            if back:
                ST(k - 1)
            if front:
                IP_za(k)
                IP_glu(k)
            if back:
                PW(k - 1)
            if front:
                GL(k)
                IP_zb(k)
            if back:
                OPF(k - 1)

        if DBG:
            P.op("sp", lambda e: e.dma_start(out=dbg[:, 0:24], in_=modcol[:, 0:24]), reads=[b_modcol], dma=True, stream="dbg0")
            P.op("pool", lambda e: e.dma_start(out=dbg[0:1, 1024:2048], in_=bprow[0:1, :]), reads=[b_bprow], dma=True, stream="dbg1")
            P.op("sp", lambda e: e.dma_start(out=dbg[:, 2048:4096], in_=gpbt[:, :]), reads=[b_gpbt], dma=True, stream="dbg2")
            P.op("sp", lambda e: e.dma_start(out=dbg[:, 32:32 + NCOL], in_=colt[:, :]), reads=[b_colt], dma=True, stream="dbg3")
        P.op("sp", None, reads=b_fbuf, writes=b_fbuf)
        P.emit()
        build_nc.last_prog = P
    return nc


_NC_CACHE = {}


def _pack_inputs(inp, core):
    b, half = core // 2, core % 2
    f32 = np.float32
    x = np.asarray(inp["x"], dtype=f32)
    xe = np.zeros((HL + TOK, D), dtype=f32)
    t0 = half * TOK
    xe[HL:] = x[b, t0:t0 + TOK]
    if half == 1:
        xe[:HL] = x[b, t0 - HL:t0]
    cols = np.zeros((128, NCOL), dtype=f32)
    cols[:, C_C:C_C + 8] = np.asarray(inp["c"], f32)[b].reshape(8, 128).T
    cols[:, C_BIN:C_BIN + 20] = np.asarray(inp["b_in"], f32).reshape(20, 128).T
    cols[:, C_BPOOL:C_BPOOL + 4] = np.asarray(inp["b_pool"], f32).T
    cols[:, C_LS:C_LS + 4] = np.asarray(inp["ls_pool"], f32).reshape(4, 128).T
    wd = np.asarray(inp["w_dw"], f32)[:, 0, :]
    cols[:, C_WDW:C_WDW + 124] = wd.T.reshape(4, 128, KW).transpose(1, 0, 2).reshape(128, 124)
    cols[:, C_BDW:C_BDW + 4] = np.asarray(inp["b_dw"], f32).reshape(4, 128).T
    cols[:, C_LNG:C_LNG + 4] = np.asarray(inp["ln_conv_g"], f32).reshape(4, 128).T
    cols[:, C_LNB:C_LNB + 4] = np.asarray(inp["ln_conv_b"], f32).reshape(4, 128).T
    cols[:, C_BPW:C_BPW + 4] = np.asarray(inp["b_pw"], f32).reshape(4, 128).T
    cols[:, C_MASK] = 1.0 if half == 1 else 0.0
    tg = t0 + np.arange(16)
    for g, w in enumerate(POOLW):
        cols[:, C_INVC + 16 * g:C_INVC + 16 * g + 16] = (1.0 / np.minimum(tg + 1, w)).astype(f32)[None, :]
    rows = np.concatenate([np.asarray(inp["b_ada"], f32), np.asarray(inp["b_out"], f32)]).reshape(4, 1024)
    gpb = np.concatenate([np.broadcast_to(np.asarray(inp["ln_post_g"], f32)[None, :], (128, D)),
                          np.broadcast_to(np.asarray(inp["ln_post_b"], f32)[None, :], (128, D))], axis=1)
    return {
        "x": np.ascontiguousarray(xe),
        "cols": cols,
        "rows": np.ascontiguousarray(rows),
        "gpb": np.ascontiguousarray(gpb),
        "w_ada": np.ascontiguousarray(np.asarray(inp["w_ada"], f32)),
        "w_in": np.ascontiguousarray(np.asarray(inp["w_in"], f32)),
        "w_pool": np.ascontiguousarray(np.asarray(inp["w_pool"], f32)),
        "w_pw": np.ascontiguousarray(np.asarray(inp["w_pw"], f32)),
        "w_out": np.ascontiguousarray(np.asarray(inp["w_out"], f32)),
    }


def kernel(**inputs):
    if "nc" not in _NC_CACHE:
        _NC_CACHE["nc"] = build_nc()
    nc = _NC_CACHE["nc"]
    in_maps = [_pack_inputs(inputs, i) for i in range(NCORES)]
    res = run_bass_kernel_spmd(nc, in_maps, core_ids=list(range(NCORES)))
    out = np.empty((4, SEQ, D), dtype=np.float32)
    for i in range(NCORES):
        b, half = i // 2, i % 2
        out[b, half * TOK:(half + 1) * TOK] = res.results[i]["y"]
    return out
```

```python
import numpy as np
from contextlib import ExitStack
import concourse.bass as bass
import concourse.mybir as mybir
from concourse.bass_utils import run_bass_kernel_spmd

F32 = mybir.dt.float32
BF16 = mybir.dt.bfloat16
AF = mybir.ActivationFunctionType
ALU = mybir.AluOpType
AX = mybir.AxisListType

D = 1024
DIN = 2560
SEQ = 8192
NCORES = 8
TOK = 4096
TT = 512
NSUB = TT // 128
NT = TOK // TT
HL = 32
W = HL + TT
KW = 31
ALPHA = float(2.0 ** 0.25)
EPS = 1e-5
POOLW = (2, 4, 8, 16)
PRIO_TR, PRIO_ST, PRIO_CV = 0.0, 0.0, 0.0
NDVE = 10
NPE = KW - NDVE

C_C, C_BIN, C_BPOOL, C_LS, C_WDW, C_BDW, C_LNG, C_LNB, C_BPW, C_MASK, C_INVC = (
    0, 8, 28, 32, 36, 160, 164, 168, 172, 176, 177)
NCOL = 177 + 64


ALL_BUFS = []


class Buf:
    __slots__ = ("name", "last_w", "readers", "hist", "excl")

    def __init__(self, name, excl=False):
        ALL_BUFS.append(self)
        self.name = name
        self.excl = excl
        self.last_w = None
        self.readers = []
        self.hist = []


class Op:
    __slots__ = ("eng", "fn", "deps", "dma", "stream", "signal", "count",
                 "idx", "eidx", "waits", "lane", "name", "semkey", "cost", "xfer", "alldeps",
                 "t0", "t1", "rank", "clk", "after_issue", "cls", "prio")


class _Probe:
    def __init__(self):
        self.calls = []

    def __getattr__(self, name):
        def f(*a, **kw):
            self.calls.append((name, a, kw))
            return self
        return f


def _fsize(ap):
    n = 1
    for d in tuple(ap.shape)[1:]:
        n *= int(d)
    return n


_ACT_CLS = {}


def _act_class(fn):
    p = _Probe()
    fn(p)
    for name, a, kw in p.calls:
        if name == "activation":
            f = kw.get("func")
            if f == AF.Silu:
                return "silu"
            if f == AF.Sigmoid:
                return "sigmoid"
            if f == AF.Sqrt:
                return "sqrt"
    return None


def _estimate(eng, fn, dma):
    if fn is None:
        return 0.01, 0.0
    p = _Probe()
    fn(p)
    cost, xfer = 0.0, 0.0
    for name, a, kw in p.calls:
        if name == "dma_start":
            src = kw["in_"]
            nbytes = int(src.shape[0]) * _fsize(src) * 4
            cost += 0.1 if eng == "sp" else 0.6
            xfer += 0.3 + nbytes / 330e3
        elif name == "matmul":
            cost += 0.003 + 0.000448 * _fsize(kw["rhs"])
        elif name == "transpose":
            cost += 0.095
        elif name == "activation":
            cost += 0.22 + 0.00095 * _fsize(kw["in_"])
        elif eng == "pool":
            if kw.get("op", None) == ALU.pow:
                cost += 0.5
            else:
                ap = kw.get("out", a[0] if a else None)
                cost += 0.25 + 0.0017 * _fsize(ap)
        else:
            ap = kw.get("in_", kw.get("in0", kw.get("out", a[0] if a else None)))
            n = _fsize(ap)
            if name == "reciprocal":
                cost += 0.1 + 0.0065 * n
            elif name == "bn_aggr":
                cost += 0.15
            else:
                cost += 0.14 + 0.00115 * n
    if eng in ("act", "dve", "pool") and not dma:
        cost += 0.12 * max(1, len(p.calls))
    return cost, xfer


class Prog:
    ENGS = ("pe", "act", "dve", "pool", "sp")

    def __init__(self, nc):
        self.nc = nc
        self.ops = []
        self.per_eng = {e: [] for e in self.ENGS}
        self._uid = 0

    def op(self, eng, fn, reads=(), writes=(), dma=False, stream=None, name="", after_issue=(), after=(),
           prio=0.0):
        o = Op()
        o.prio = prio
        o.after_issue = [a for a in after_issue if a is not None]
        o.eng, o.fn, o.dma, o.name = eng, fn, dma, name
        o.cost, o.xfer = _estimate(eng, fn, dma)
        o.cls = _act_class(fn) if (eng == "act" and fn is not None) else None
        o.deps = {}
        o.signal = False
        o.count = 0
        o.waits = []
        o.idx = len(self.ops)
        o.eidx = len(self.per_eng[eng])
        if dma:
            self._uid += 1
            o.lane = "dma%d" % self._uid
            if stream is None:
                stream = (list(writes) + list(reads))[0].name
            o.stream = stream
        else:
            o.lane = eng
            o.stream = None
        writes = list(writes) + [b for b in reads if b.excl and b not in writes]
        for b in reads:
            if b.last_w is not None:
                o.deps[b.last_w] = True
        for b in writes:
            if b.last_w is not None:
                o.deps.setdefault(b.last_w, False)
            for r in b.readers:
                if r is not o:
                    o.deps.setdefault(r, False)
        for b in reads:
            b.readers.append(o)
            b.hist.append((o, "r"))
        for b in writes:
            b.last_w = o
            b.readers = []
            b.hist.append((o, "w"))
        for a in after:
            o.deps.setdefault(a, False)
        o.deps.pop(o, None)
        self.ops.append(o)
        self.per_eng[eng].append(o)
        return o

    @staticmethod
    def _needs_sync(o, d, raw):
        return True

    def _schedule(self):
        import heapq
        ops = self.ops
        succ = {o: [] for o in ops}
        for o in ops:
            o.alldeps = list(o.deps.keys())
            for d in o.alldeps:
                succ[d].append(o)
        for o in reversed(ops):
            r = 0.0
            for s_ in succ[o]:
                if s_.rank > r:
                    r = s_.rank
            o.rank = r + o.cost + o.xfer + o.prio
        LAT = 0.4
        TBL = 1.3
        act_tbl = [None]
        isucc = {o: [] for o in ops}
        for o in ops:
            for a in o.after_issue:
                assert a.eng == o.eng
                isucc[a].append(o)
        ndeps = {o: len(o.alldeps) + len(o.after_issue) for o in ops}
        ready_t = {o: 0.0 for o in ops}
        ready = {e: [] for e in self.ENGS}
        for o in ops:
            if ndeps[o] == 0:
                heapq.heappush(ready[o.eng], (-o.rank, o.idx, o))
        eng_free = {e: 0.0 for e in self.ENGS}
        dma_free = {e: 0.0 for e in self.ENGS}
        events = []
        t = 0.0
        done = 0
        n = len(ops)
        while done < n:
            progressed = False
            for e in self.ENGS:
                if eng_free[e] > t + 1e-9 or not ready[e]:
                    continue
                cand = [c for c in ready[e] if ready_t[c[2]] <= t + 1e-9]
                if not cand:
                    continue
                extra = 0.0
                if e == "act":
                    same = [c for c in cand if c[2].cls is None or c[2].cls == act_tbl[0]]
                    if same:
                        cand = same
                best = min(cand)
                ready[e].remove(best)
                heapq.heapify(ready[e])
                o = best[2]
                if e == "act" and o.cls is not None and o.cls != act_tbl[0]:
                    act_tbl[0] = o.cls
                    extra = TBL
                o.t0 = t
                eng_free[e] = t + o.cost + extra
                if o.dma:
                    st = max(t + o.cost, dma_free[e])
                    dma_free[e] = st + o.xfer
                    o.t1 = dma_free[e]
                else:
                    o.t1 = t + o.cost + extra
                heapq.heappush(events, (o.t1, o.idx, o))
                for s_ in isucc[o]:
                    ndeps[s_] -= 1
                    if eng_free[e] > ready_t[s_]:
                        ready_t[s_] = eng_free[e]
                    if ndeps[s_] == 0:
                        heapq.heappush(ready[s_.eng], (-s_.rank, s_.idx, s_))
                progressed = True
            if progressed:
                continue
            cands = []
            if events:
                cands.append(events[0][0])
            for e in self.ENGS:
                if ready[e]:
                    tr = min(ready_t[c[2]] for c in ready[e])
                    cands.append(max(tr, eng_free[e]))
            tn = min(cands)
            if tn <= t + 1e-9:
                tn = t + 1e-3
            t = tn
            while events and events[0][0] <= t + 1e-9:
                _, _, o = heapq.heappop(events)
                done += 1
                for s_ in succ[o]:
                    ndeps[s_] -= 1
                    rt = o.t1 + (LAT if (s_.eng != o.eng or o.dma) else 0.0)
                    if rt > ready_t[s_]:
                        ready_t[s_] = rt
                    if ndeps[s_] == 0:
                        heapq.heappush(ready[s_.eng], (-s_.rank, s_.idx, s_))
        self.sim_span = max(o.t1 for o in ops)
        self.ops = sorted(ops, key=lambda o: (o.t0, o.idx))
        self.per_eng = {e: [] for e in self.ENGS}
        for i, o in enumerate(self.ops):
            o.idx = i
            o.eidx = len(self.per_eng[o.eng])
            self.per_eng[o.eng].append(o)

    def _finalize(self):
        self._schedule()
        for o in self.ops:
            o.semkey = ("dma", o.stream) if o.dma else ("eng", o.eng)
        for o in self.ops:
            o.deps = [d for d, raw in o.deps.items() if self._needs_sync(o, d, raw)]
            for d in o.deps:
                d.signal = True
        cnt = {}
        clock = {e: {} for e in self.ENGS}
        snap = {}
        for o in self.ops:
            ck = clock[o.eng]
            waits = {}
            for d in o.deps:
                key = d.semkey
                if ck.get(key, 0) >= d.count:
                    continue
                if waits.get(key, 0) < d.count:
                    waits[key] = d.count
            for key, c in waits.items():
                for k2, c2 in snap[(key, c)].items():
                    if ck.get(k2, 0) < c2:
                        ck[k2] = c2
            o.waits = sorted(waits.items())
            o.clk = dict(ck)
            if o.signal or o.dma:
                key = o.semkey
                cnt[key] = cnt.get(key, 0) + (16 if o.dma else 1)
                o.count = cnt[key]
                s = dict(ck)
                s[key] = o.count
                snap[(key, o.count)] = s
            else:
                o.semkey = None

    def emit(self):
        nc = self.nc
        self._finalize()
        keys = []
        seen = set()
        for o in self.ops:
            if o.semkey is not None and o.semkey not in seen:
                seen.add(o.semkey)
                keys.append(o.semkey)
        self.n_sems = len(keys)
        with ExitStack() as es:
            semh = {}
            for i, k in enumerate(keys):
                semh[k] = es.enter_context(nc.semaphore("s%d" % i))
            block = es.enter_context(nc.Block())

            def run(engname):
                def body(e):
                    for o in self.per_eng[engname]:
                        for key, c in o.waits:
                            e.wait_ge(semh[key], c)
                        ins = o.fn(e) if o.fn is not None else None
                        if o.semkey is not None:
                            if ins is None:
                                raise RuntimeError("signal on empty op " + o.name)
                            ins.then_inc(semh[o.semkey], 16 if o.dma else 1)
                return body

            block.tensor(run("pe"))
            block.scalar(run("act"))
            block.vector(run("dve"))
            block.gpsimd(run("pool"))
            block.sync(run("sp"))


def build_nc():
    del ALL_BUFS[:]
    nc = bass.Bass("TRN2", target_bir_lowering=False)

    def din(name, shape):
        return nc.dram_tensor(name, shape, F32, kind="ExternalInput").ap()

    x = din("x", [HL + TOK, D])
    cols = din("cols", [128, NCOL])
    rows = din("rows", [4, 1024])
    gpb = din("gpb", [128, 2 * D])
    w_ada = din("w_ada", [D, 3 * D])
    w_in = din("w_in", [D, DIN])
    w_pool = din("w_pool", [4, 128, 128])
    w_pw = din("w_pw", [512, 512])
    w_out = din("w_out", [D, D])
    y = nc.dram_tensor("y", [TOK, D], F32, kind="ExternalOutput").ap()
    import os
    DBG = os.environ.get("KDBG") == "1"
    if DBG:
        dbg = nc.dram_tensor("dbg", [128, 4096], F32, kind="ExternalOutput").ap()

    with ExitStack() as es:
        def sb(name, shape, dt):
            return es.enter_context(nc.sbuf_tensor(name, shape, dt))

        def ps(name, shape, dt):
            return es.enter_context(nc.psum_tensor(name, shape, dt))

        P = Prog(nc)

        colt = sb("colt", [128, NCOL], F32); b_colt = Buf("colt")
        bprow = sb("bprow", [1, D], BF16); b_bprow = Buf("bprow")
        gpbt = sb("gpbt", [128, 2 * D], F32); b_gpbt = Buf("gpbt")
        identf = sb("identf", [128, 128], F32); b_identf = Buf("identf")
        ident = sb("ident", [128, 128], BF16); b_ident = Buf("ident")
        onesbf = sb("onesbf", [128, 128], BF16); b_onesbf = Buf("onesbf")
        ones512 = sb("ones512", [128, 128], BF16); b_ones512 = Buf("ones512")
        onesrow = sb("onesrow", [1, 128], BF16); b_onesrow = Buf("onesrow")
        epst = sb("epst", [128, 1], F32); b_epst = Buf("epst")
        nhalf = sb("nhalf", [128, 1], F32); b_nhalf = Buf("nhalf")
        siluc = sb("siluc", [128, 8], F32); b_siluc = Buf("siluc")
        modcol = sb("modcol", [128, 24], F32); b_modcol = Buf("modcol")
        bmask = sb("bmask", [128, 4], F32); b_bmask = Buf("bmask")

        win = sb("win", [128, 8, DIN], BF16); b_win = [Buf("win%d" % i) for i in range(5)]
        wout = sb("wout", [128, 8, D], BF16); b_wout = [Buf("wout%d" % i) for i in range(8)]
        wpw = sb("wpw", [128, 4, 512], BF16); b_wpw = Buf("wpw")
        wpool = sb("wpool", [128, 4, 128], BF16); b_wpool = Buf("wpool")
        diag = sb("diag", [128, 4, NPE, 128], BF16); b_diag = [Buf("diag%d" % i) for i in range(4)]

        xin = [sb("xin%d" % i, [128, D], F32) for i in range(3)]
        b_xin = [Buf("xin%d" % i) for i in range(3)]
        lst = [sb("lst%d" % i, [128, 2, 6], F32) for i in range(3)]
        lmv = [sb("lmv%d" % i, [128, 2], F32) for i in range(3)]
        lve = [sb("lve%d" % i, [128, 1], F32) for i in range(3)]
        lrs = [sb("lrs%d" % i, [128, 1], F32) for i in range(3)]
        lnm = [sb("lnm%d" % i, [128, 1], F32) for i in range(3)]
        b_lst = [Buf("lst%d" % i) for i in range(3)]
        b_lmv = [Buf("lmv%d" % i) for i in range(3)]
        b_lve = [Buf("lve%d" % i) for i in range(3)]
        b_lrs = [Buf("lrs%d" % i) for i in range(3)]
        b_lnm = [Buf("lnm%d" % i) for i in range(3)]
        xn = sb("xn", [128, NSUB, D], BF16); b_xn = [Buf("xn%d" % i) for i in range(NSUB)]
        hTs = [sb("hT%d" % i, [128, 8, TT], BF16) for i in range(2)]
        b_hTs = [[Buf("hT%d_%d" % (i, c)) for c in range(8)] for i in range(2)]
        hTh = sb("hTh", [128, 8, HL], BF16); b_hTh = [Buf("hTh%d" % i) for i in range(8)]
        ubuf = sb("ubuf", [128, 4, W], F32); b_ubuf = [Buf("ubuf%d" % i) for i in range(4)]
        scr = [sb("scr%d" % i, [128, W], F32) for i in range(2)]
        b_scr = [Buf("scr%d" % i) for i in range(2)]
        b_pooled = [Buf("pooled%d" % i) for i in range(4)]
        b_sza = [Buf("sza%d" % i) for i in range(4)]
        b_szb = [Buf("szb%d" % i) for i in range(4)]
        b_sig = [Buf("sig%d" % i) for i in range(4)]
        vbuf = sb("vbuf", [128, 4, W], BF16); b_vbuf = [Buf("vbuf%d" % i) for i in range(4)]
        b_cbf = [Buf("cbf%d" % i) for i in range(4)]
        b_csq = [Buf("csq%d" % i) for i in range(4)]
        msq = sb("msq", [128, TT], F32); b_msq = Buf("msq")
        rstd = sb("rstd", [128, TT], F32); b_rstd = Buf("rstd")
        b_lnv = [Buf("lnv%d" % i) for i in range(4)]
        arena = sb("arena", [128, 10 * 4 * TT], BF16)

        def av(i):
            return arena[:, i * 4 * TT:(i + 1) * 4 * TT].rearrange("p (a b) -> p a b", b=TT)
        sza, szb, sig, pooled, cbf, csq, yb, lnv = [av(i) for i in range(8)]
        ya = [av(8), av(9)]
        b_ya = [[Buf("ya%d_%d" % (i, j)) for j in range(4)] for i in range(2)]
        b_yb = [Buf("yb%d" % i) for i in range(4)]
        NF = 3
        fbuf = [sb("fbuf%d" % i, [128, D], F32) for i in range(NF)]
        b_fbuf = [Buf("fbuf%d" % i) for i in range(NF)]
        fst = [sb("fst%d" % i, [128, 2, 6], F32) for i in range(NF)]
        fmv = [sb("fmv%d" % i, [128, 2], F32) for i in range(NF)]
        fve = [sb("fve%d" % i, [128, 1], F32) for i in range(NF)]
        frs = [sb("frs%d" % i, [128, 1], F32) for i in range(NF)]
        fnm = [sb("fnm%d" % i, [128, 1], F32) for i in range(NF)]
        b_fst = [Buf("fst%d" % i) for i in range(NF)]
        b_fmv = [Buf("fmv%d" % i) for i in range(NF)]
        b_fve = [Buf("fve%d" % i) for i in range(NF)]
        b_frs = [Buf("frs%d" % i) for i in range(NF)]
        b_fnm = [Buf("fnm%d" % i) for i in range(NF)]
        tmp16 = sb("tmp16", [128, 32], F32); b_tmp16 = Buf("tmp16")

        pT = [ps("pT%d" % i, [128, 1024], BF16) for i in range(1)]
        b_pT = [Buf("pT%d" % i, excl=True) for i in range(1)]
        NB = 7
        pF = [ps("pF%d" % i, [128, 512], F32) for i in range(NB)]
        b_pF = [Buf("pF%d" % i, excl=True) for i in range(NB)]
        bank_ctr = [0]

        NQ = 5
        ctr2 = [0, 0]

        def nb(long=False):
            if long:
                i = NQ + ctr2[1] % (NB - NQ)
                ctr2[1] += 1
            else:
                i = ctr2[0] % NQ
                ctr2[0] += 1
            return pF[i], b_pF[i]

        def col(c, n=1):
            return colt[:, c:c + n]

        WA = 3 * D
        wa = [arena[:, 0:WA], arena[:, WA:2 * WA], arena[:, 2 * WA:3 * WA], arena[:, 3 * WA:4 * WA],
              arena[:, 4 * WA:5 * WA],
              hTs[1][:].rearrange("p a b -> p (a b)")[:, 0:WA],
              wout[:].rearrange("p a b -> p (a b)")[:, 0:WA],
              wout[:].rearrange("p a b -> p (a b)")[:, WA:2 * WA]]
        wa_bufs = [b_sza + b_szb[0:2], b_szb[2:4] + b_sig, b_pooled + b_cbf[0:2], b_cbf[2:4] + b_csq,
                   b_yb + b_lnv[0:2], b_hTs[1], b_wout[0:3], b_wout[3:6]]
        yav = [arena[:, (8 + i) * 4 * TT:(9 + i) * 4 * TT] for i in range(2)]
        rowbf_bufs = b_ya[0] + b_ya[1]
        rep = msq[:].bitcast(BF16).rearrange("p (a b) -> p a b", b=128)
        rep_bufs = [b_msq]
        boutst = gpbt
        boutst_bufs = [b_gpbt]

        xin_ctr = [0]
        x_load_ops = []

        def ln_rstd(mv, ve, rs, nm, b_mv, b_ve, b_rs, b_nm, np_, use_pool=True):
            if use_pool:
                P.op("pool", lambda e: e.tensor_scalar(out=ve[0:np_, :], in0=mv[0:np_, 1:2], scalar1=EPS,
                                                       scalar2=None, op0=ALU.add),
                     reads=[b_mv], writes=[b_ve])
                P.op("pool", lambda e: e.tensor_tensor(out=rs[0:np_, :], in0=ve[0:np_, :], in1=nhalf[0:np_, :],
                                                       op=ALU.pow),
                     reads=[b_ve, b_nhalf], writes=[b_rs])
            else:
                P.op("act", lambda e: e.activation(out=ve[0:np_, :], in_=mv[0:np_, 1:2], func=AF.Sqrt,
                                                   bias=epst[0:np_, :], scale=1.0),
                     reads=[b_mv, b_epst], writes=[b_ve])
                P.op("dve", lambda e: e.reciprocal(out=rs[0:np_, :], in_=ve[0:np_, :]),
                     reads=[b_ve], writes=[b_rs])
            P.op("dve", lambda e: e.scalar_tensor_tensor(out=nm[0:np_, :], in0=mv[0:np_, 0:1], scalar=-1.0,
                                                         in1=rs[0:np_, :], op0=ALU.mult, op1=ALU.mult),
                 reads=[b_mv, b_rs], writes=[b_nm])

        xnh = scr[0][:].bitcast(BF16)[0:HL, 0:D]

        def ln1_sub(row0, np_, s, use_pool=True, halo=False, prio=0.0):
            dst = xnh if halo else xn[0:np_, s, :]
            dst_bufs = [b_scr[0]] if halo else [b_xn[s]]
            r = xin_ctr[0] % 3
            xin_ctr[0] += 1
            x_load_ops.append(P.op("sp", lambda e: e.dma_start(out=xin[r][0:np_, :], in_=x[row0:row0 + np_, :]),
                                   writes=[b_xin[r]], dma=True))

            def stats(e):
                e.bn_stats(out=lst[r][0:np_, 0, :], in_=xin[r][0:np_, 0:512])
                return e.bn_stats(out=lst[r][0:np_, 1, :], in_=xin[r][0:np_, 512:1024])
            P.op("dve", stats, reads=[b_xin[r]], writes=[b_lst[r]])
            P.op("dve", lambda e: e.bn_aggr(out=lmv[r][0:np_, :],
                                            in_=lst[r][0:np_].rearrange("p a b -> p (a b)")),
                 reads=[b_lst[r]], writes=[b_lmv[r]])
            ln_rstd(lmv[r], lve[r], lrs[r], lnm[r], b_lmv[r], b_lve[r], b_lrs[r], b_lnm[r], np_, use_pool)
            P.op("act", lambda e: e.activation(out=dst, in_=xin[r][0:np_, :], func=AF.Identity,
                                               bias=lnm[r][0:np_, :], scale=lrs[r][0:np_, :]),
                 reads=[b_xin[r], b_lrs[r], b_lnm[r]], writes=dst_bufs, prio=prio)

        def LN1(k):
            for s in range(NSUB):
                ln1_sub(HL + k * TT + s * 128, 128, s, use_pool=(k >= 2), prio=(60.0 if k == 1 else 0.0))

        def MODT():
            hT, b_hT = hTs[0], b_hTs[0]
            for c in range(8):
                P.op("dve", lambda e, c=c: e.tensor_scalar(out=hT[:, c, :], in0=hT[:, c, :],
                                                          scalar1=modcol[:, 16 + c:17 + c], scalar2=modcol[:, c:c + 1],
                                                          op0=ALU.mult, op1=ALU.add),
                     reads=[b_hT[c], b_modcol], writes=[b_hT[c]])
                P.op("dve", lambda e, c=c: e.tensor_scalar(out=hTh[:, c, :], in0=hTh[:, c, :],
                                                          scalar1=modcol[:, 16 + c:17 + c], scalar2=modcol[:, c:c + 1],
                                                          op0=ALU.mult, op1=ALU.add),
                     reads=[b_hTh[c], b_modcol], writes=[b_hTh[c]])

        def TR(k, raw=False):
            hT, b_hT = hTs[k % 2], b_hTs[k % 2]
            for q in range(4):
                bank, bb = pT[0], b_pT[0]

                def tr(e, q=q, bank=bank):
                    for c in (2 * q, 2 * q + 1):
                        for s in range(NSUB):
                            ins = e.transpose(out=bank[:, (c % 2) * 512 + s * 128:(c % 2) * 512 + (s + 1) * 128],
                                              in_=xn[:, s, c * 128:(c + 1) * 128], identity=ident[:])
                    return ins
                P.op("pe", tr, reads=b_xn + [b_ident], writes=[bb], name="TR%d.%d" % (k, q), prio=PRIO_TR)
                for c in (2 * q, 2 * q + 1):
                    if raw:
                        P.op("act", lambda e, c=c, bank=bank: e.activation(
                            out=hT[:, c, :], in_=bank[:, (c % 2) * 512:(c % 2 + 1) * 512], func=AF.Identity),
                            reads=[bb], writes=[b_hT[c]])
                    else:
                        P.op("act", lambda e, c=c, bank=bank: e.activation(
                            out=hT[:, c, :], in_=bank[:, (c % 2) * 512:(c % 2 + 1) * 512], func=AF.Identity,
                            bias=modcol[:, c:c + 1], scale=modcol[:, 16 + c:17 + c]),
                            reads=[bb, b_modcol], writes=[b_hT[c]])

        P.op("sp", lambda e: e.dma_start(out=colt[:], in_=cols[:, :]), writes=[b_colt], dma=True)
        for q in range(3):
            P.op("sp", lambda e, q=q: e.dma_start(out=fbuf[q][0:1, :], in_=rows[q:q + 1, :]),
                 writes=[b_fbuf[q]], dma=True)
        P.op("sp", lambda e: e.dma_start(out=boutst[0:1, 0:D], in_=rows[3:4, :]), writes=boutst_bufs, dma=True)

        P.op("pool", lambda e: e.memset(identf[:], 0.0), writes=[b_identf])
        P.op("pool", lambda e: e.affine_select(out=identf[:], in_=identf[:], pattern=[[-1, 128]],
                                               compare_op=ALU.not_equal, fill=1.0, base=0,
                                               channel_multiplier=1),
             reads=[b_identf], writes=[b_identf])
        P.op("dve", lambda e: e.tensor_copy(out=ident[:], in_=identf[:]), reads=[b_identf], writes=[b_ident])
        P.op("dve", lambda e: e.memset(onesbf[:], 1.0), writes=[b_onesbf])
        P.op("dve", lambda e: e.memset(ones512[:], 1.0 / 512.0), writes=[b_ones512])
        P.op("dve", lambda e: e.memset(onesrow[:], 1.0), writes=[b_onesrow])
        P.op("dve", lambda e: e.memset(epst[:], EPS), writes=[b_epst])
        P.op("dve", lambda e: e.memset(nhalf[:], -0.5), writes=[b_nhalf])

        ln1_sub(0, HL, 0, use_pool=False, halo=True)
        LN1(0)
        def tr_h(e):
            for c in range(8):
                ins = e.transpose(out=pT[0][:, c * HL:(c + 1) * HL], in_=xnh[:, c * 128:(c + 1) * 128],
                                  identity=ident[0:HL, 0:HL])
            return ins
        P.op("pe", tr_h, reads=[b_scr[0], b_ident], writes=[b_pT[0]])
        for c in range(8):
            P.op("act", lambda e, c=c: e.activation(out=hTh[:, c, :], in_=pT[0][:, c * HL:(c + 1) * HL],
                                                  func=AF.Identity),
                 reads=[b_pT[0]], writes=[b_hTh[c]])
        TR(0, raw=True)

        P.op("act", lambda e: e.activation(out=siluc[:], in_=col(C_C, 8), func=AF.Silu),
             reads=[b_colt], writes=[b_siluc])

        def mk_rep(e):
            for k in range(8):
                ins = e.tensor_scalar(out=rep[:, k, :], in0=onesbf[:], scalar1=siluc[:, k:k + 1],
                                      scalar2=None, op0=ALU.mult)
            return ins
        P.op("dve", mk_rep, reads=[b_onesbf, b_siluc], writes=rep_bufs)

        def mk_rowbf(e):
            for q in range(3):
                ins = e.tensor_copy(out=yav[q // 2][0:1, (q % 2) * 1024:(q % 2 + 1) * 1024], in_=fbuf[q][0:1, :])
            return ins
        P.op("dve", mk_rowbf, reads=b_fbuf, writes=rowbf_bufs)

        tok = []

        def throttle():
            t = Buf("tok%d" % len(tok))
            tok.append(t)
            rd = []
            return rd, [t]
        wa_ops = []
        CA = 2 * D
        for k in range(8):
            trd, twr = throttle()
            wa_ops.append(P.op("pool", lambda e, k=k: e.dma_start(out=wa[k][:, 0:CA],
                                                                in_=w_ada[k * 128:(k + 1) * 128, 0:CA]),
                               reads=trd, writes=wa_bufs[k] + twr, dma=True, stream="waA%d" % k,
                               after_issue=wa_ops[-1:], after=(x_load_ops[0:3] if k < 2 else ())))

            def mm_modA(e, k=k):
                for j in range(4):
                    ins = e.matmul(pF[j][:], lhsT=rep[:, k, :], rhs=wa[k][:, j * 512:(j + 1) * 512],
                                   start=(k == 0), stop=False)
                return ins
            P.op("pe", mm_modA, reads=rep_bufs + wa_bufs[k], writes=b_pF[0:4])

        def mm_modbA(e):
            for j in range(4):
                q = j // 2
                c0 = (q % 2) * 1024 + (j % 2) * 512
                ins = e.matmul(pF[j][:], lhsT=onesbf[0:1, :], rhs=yav[q // 2][0:1, c0:c0 + 512],
                               start=False, stop=True)
            return ins
        P.op("pe", mm_modbA, reads=[b_onesbf] + rowbf_bufs, writes=b_pF[0:4])

        trd, twr = throttle()
        win_first = P.op("pool", lambda e: e.dma_start(
            out=win[:, :, 0:512], in_=w_in[:, 0:512].rearrange("(k p) n -> p k n", p=128)),
            reads=trd, writes=[b_win[0]] + twr, dma=True, after_issue=[wa_ops[7]])

        wb_ops = [win_first]
        for k in range(8):
            trd, twr = throttle()
            wb_ops.append(P.op("pool", lambda e, k=k: e.dma_start(out=wa[k][:, CA:WA],
                                                                in_=w_ada[k * 128:(k + 1) * 128, CA:WA]),
                               reads=trd, writes=wa_bufs[k] + twr, dma=True, stream="waB%d" % k,
                               after_issue=wb_ops[-1:]))

            def mm_modB(e, k=k):
                for j in (4, 5):
                    ins = e.matmul(pF[j][:], lhsT=rep[:, k, :], rhs=wa[k][:, j * 512:(j + 1) * 512],
                                   start=(k == 0), stop=False)
                return ins
            P.op("pe", mm_modB, reads=rep_bufs + wa_bufs[k], writes=b_pF[4:6])

        def mm_modbB(e):
            for j in (4, 5):
                q = j // 2
                c0 = (q % 2) * 1024 + (j % 2) * 512
                ins = e.matmul(pF[j][:], lhsT=onesbf[0:1, :], rhs=yav[q // 2][0:1, c0:c0 + 512],
                               start=False, stop=True)
            return ins
        P.op("pe", mm_modbB, reads=[b_onesbf] + rowbf_bufs, writes=b_pF[4:6])
        gate_sb = [msq, rstd]
        b_gate_sb = [b_msq, b_rstd]
        for h in range(2):
            P.op("dve", lambda e, h=h: e.tensor_copy(out=gate_sb[h][:], in_=pF[4 + h][:]),
                 reads=[b_pF[4 + h]], writes=[b_gate_sb[h]])
        bank_ctr[0] = 0

        prev = wb_ops[-1]
        for g in (1, 3, 2, 4):
            trd, twr = throttle()
            prev = P.op("pool", lambda e, g=g: e.dma_start(
                out=win[:, :, g * 512:(g + 1) * 512],
                in_=w_in[:, g * 512:(g + 1) * 512].rearrange("(k p) n -> p k n", p=128)),
                reads=trd, writes=[b_win[g]] + twr, dma=True, after_issue=[prev])
        for j in range(4):
            def mk_diag(e, j=j):
                for t in range(NPE):
                    ins = e.tensor_scalar(out=diag[:, j, t, :], in0=ident[:],
                                          scalar1=col(C_WDW + j * KW + t), scalar2=None, op0=ALU.mult)
                return ins
            P.op("dve", mk_diag, reads=[b_ident, b_colt], writes=[b_diag[j]])
        P.op("dve", lambda e: e.tensor_scalar(out=bmask[:], in0=col(C_BIN, 4), scalar1=col(C_MASK),
                                              scalar2=None, op0=ALU.mult),
             reads=[b_colt], writes=[b_bmask])

        fb0v = fbuf[0][:].rearrange("p (a b) -> p a b", b=128)
        fb1v = fbuf[1][:].rearrange("p (a b) -> p a b", b=128)
        for half, fbv, bfb in ((0, fb0v, b_fbuf[0]), (1, fb1v, b_fbuf[1])):
            def ext(e, half=half, fbv=fbv):
                for jb in range(8):
                    bank = pF[2 * half + jb // 4]
                    ins = e.tensor_tensor(out=fbv[:, jb, :], in0=bank[:, (jb % 4) * 128:(jb % 4 + 1) * 128],
                                          in1=identf[:], op=ALU.mult)
                return ins
            P.op("dve", ext, reads=[b_pF[2 * half], b_pF[2 * half + 1], b_identf], writes=[bfb])
            P.op("dve", lambda e, half=half, fbv=fbv: e.tensor_reduce(
                out=modcol[:, 8 * half:8 * half + 8], in_=fbv, axis=AX.X, op=ALU.add),
                reads=[bfb], writes=[b_modcol])
        P.op("dve", lambda e: e.tensor_scalar(out=modcol[:, 16:24], in0=modcol[:, 8:16], scalar1=1.0,
                                              scalar2=None, op0=ALU.add),
             reads=[b_modcol], writes=[b_modcol])
        def mk_bprow(e):
            e.tensor_tensor(out=bprow[0:1, 0:512], in0=gate_sb[0][0:1, :], in1=boutst[0:1, 0:512], op=ALU.mult)
            return e.tensor_tensor(out=bprow[0:1, 512:1024], in0=gate_sb[1][0:1, :], in1=boutst[0:1, 512:1024],
                                   op=ALU.mult)
        P.op("dve", mk_bprow, reads=b_gate_sb + boutst_bufs, writes=[b_bprow])

        prev = P.op("pool", lambda e: e.dma_start(out=wpool[:], in_=w_pool.rearrange("g c d -> c g d")),
                    reads=[b_modcol], writes=[b_wpool], dma=True, after_issue=[prev])
        for k in range(8):
            prev = P.op("pool", lambda e, k=k: e.dma_start(out=wout[:, k, :], in_=w_out[k * 128:(k + 1) * 128, :]),
                        reads=[b_modcol], writes=[b_wout[k]], dma=True, after_issue=[prev])
        prev = P.op("pool", lambda e: e.dma_start(out=wpw[:], in_=w_pw.rearrange("(k p) n -> p k n", p=128)),
                    reads=[b_modcol], writes=[b_wpw], dma=True, after_issue=[prev])
        prev = P.op("pool", lambda e: e.dma_start(out=gpbt[:], in_=gpb[:, :]), reads=[b_modcol], writes=[b_gpbt],
                    dma=True, after_issue=[prev], stream="gpbt_sw")

        for k in range(8):
            def fold(e, k=k):
                for h in range(2):
                    if k < 4:
                        ins = e.scalar_tensor_tensor(out=wout[:, k, h * 512:(h + 1) * 512],
                                                     in0=wout[:, k, h * 512:(h + 1) * 512],
                                                     scalar=col(C_LS + k), in1=gate_sb[h][:],
                                                     op0=ALU.mult, op1=ALU.mult)
                    else:
                        ins = e.tensor_tensor(out=wout[:, k, h * 512:(h + 1) * 512],
                                              in0=wout[:, k, h * 512:(h + 1) * 512],
                                              in1=gate_sb[h][:], op=ALU.mult)
                return ins
            P.op("dve", fold, reads=[b_wout[k], b_colt] + b_gate_sb, writes=[b_wout[k]])

        def ip_group(m, n0, n, src=None, src_bufs=None):
            bank, bb = nb()

            def mm(e):
                for kc in range(8):
                    ins = e.matmul(bank[:, 0:n], lhsT=win[:, kc, m * 128:(m + 1) * 128],
                                   rhs=src[:, kc, n0:n0 + n], start=(kc == 0), stop=(kc == 7))
                return ins
            P.op("pe", mm, reads=src_bufs + [b_win[m // 4]], writes=[bb], name="IP.m%d.n%d" % (m, n))
            return bank, bb

        def halo_copy(buf, bufs):
            P.op("pool", lambda e: e.tensor_copy(out=buf[:, :, 0:HL], in_=buf[:, :, TT:TT + HL]),
                 reads=bufs, writes=bufs)

        def IP_u(k):
            if k > 0:
                halo_copy(ubuf, b_ubuf)
            for m in range(4):
                bank, bb = ip_group(m, 0, TT, hTs[k % 2], b_hTs[k % 2])
                P.op("act", lambda e, m=m, bank=bank: e.activation(
                    out=ubuf[:, m, HL:W], in_=bank[:], func=AF.Identity, bias=col(C_BIN + m), scale=1.0),
                    reads=[bb, b_colt], writes=[b_ubuf[m]])

        def IP_za(k):
            for j in range(4):
                m = 4 + j
                bank, bb = ip_group(m, 0, TT, hTs[k % 2], b_hTs[k % 2])
                P.op("act", lambda e, m=m, j=j, bank=bank: e.activation(
                    out=sza[:, j, :], in_=bank[:], func=AF.Silu, bias=col(C_BIN + m), scale=1.0),
                    reads=[bb, b_colt], writes=[b_sza[j]])

        def IP_zb(k):
            for j in range(4):
                m = 16 + j
                bank, bb = ip_group(m, 0, TT, hTs[k % 2], b_hTs[k % 2])
                P.op("act", lambda e, m=m, j=j, bank=bank: e.activation(
                    out=szb[:, j, :], in_=bank[:], func=AF.Silu, bias=col(C_BIN + m), scale=1.0),
                    reads=[bb, b_colt], writes=[b_szb[j]])

        def IP_glu(k):
            if k > 0:
                halo_copy(vbuf, b_vbuf)
            for j in range(4):
                m = 12 + j
                bank, bb = ip_group(m, 0, TT, hTs[k % 2], b_hTs[k % 2])
                P.op("act", lambda e, m=m, j=j, bank=bank: e.activation(
                    out=sig[:, j, :], in_=bank[:], func=AF.Sigmoid, bias=col(C_BIN + m), scale=1.0),
                    reads=[bb, b_colt], writes=[b_sig[j]])
            for j in range(4):
                m = 8 + j
                bank, bb = ip_group(m, 0, TT, hTs[k % 2], b_hTs[k % 2])
                P.op("dve", lambda e, m=m, j=j, bank=bank: e.scalar_tensor_tensor(
                    out=vbuf[:, j, HL:W], in0=bank[:], scalar=col(C_BIN + m), in1=sig[:, j, :],
                    op0=ALU.add, op1=ALU.mult),
                    reads=[bb, b_colt, b_sig[j]], writes=[b_vbuf[j]])

        def POOLING(k):
            for g, w in enumerate(POOLW):
                A, bA = scr[0], b_scr[0]
                Bt, bB = scr[1], b_scr[1]
                src = ubuf[:, g, :]
                P.op("pool", lambda e, A=A, src=src: e.tensor_tensor(
                    out=A[:, 1:W], in0=src[:, 1:W], in1=src[:, 0:W - 1], op=ALU.add),
                    reads=[b_ubuf[g]], writes=[bA])
                S, bS = A, bA
                if w >= 4:
                    P.op("pool", lambda e, A=A, Bt=Bt: e.tensor_tensor(
                        out=Bt[:, 3:W], in0=A[:, 3:W], in1=A[:, 1:W - 2], op=ALU.add),
                        reads=[bA], writes=[bB])
                    S, bS = Bt, bB
                if w >= 8:
                    P.op("pool", lambda e, A=A, Bt=Bt: e.tensor_tensor(
                        out=A[:, 7:W], in0=Bt[:, 7:W], in1=Bt[:, 3:W - 4], op=ALU.add),
                        reads=[bB], writes=[bA])
                    S, bS = A, bA
                if w >= 16:
                    P.op("pool", lambda e, A=A, Bt=Bt: e.tensor_tensor(
                        out=Bt[:, 15:W], in0=A[:, 15:W], in1=A[:, 7:W - 8], op=ALU.add),
                        reads=[bA], writes=[bB])
                    S, bS = Bt, bB
                P.op("dve", lambda e, S=S, g=g, w=w, src=src: e.scalar_tensor_tensor(
                    out=pooled[:, g, :], in0=S[:, HL:W], scalar=1.0 / w, in1=src[:, HL:W],
                    op0=ALU.mult, op1=ALU.subtract),
                    reads=[bS, b_ubuf[g]], writes=[b_pooled[g]])
                if k == 0:
                    P.op("dve", lambda e, S=S, g=g: e.tensor_tensor(
                        out=tmp16[:, 0:16], in0=S[:, HL:HL + 16],
                        in1=colt[:, C_INVC + 16 * g:C_INVC + 16 * g + 16], op=ALU.mult),
                        reads=[bS, b_colt], writes=[b_tmp16])
                    P.op("dve", lambda e, g=g, src=src: e.tensor_tensor(
                        out=pooled[:, g, 0:16], in0=tmp16[:, 0:16], in1=src[:, HL:HL + 16], op=ALU.subtract),
                        reads=[b_tmp16, b_ubuf[g]], writes=[b_pooled[g]])

        def GL(k):
            par = k % 2
            for g in range(4):
                bank, bb = nb()
                P.op("pe", lambda e, g=g, bank=bank: e.matmul(bank[:], lhsT=wpool[:, g, :], rhs=pooled[:, g, :],
                                                             start=True, stop=True),
                     reads=[b_wpool, b_pooled[g]], writes=[bb], name="GL%d" % k)
                P.op("dve", lambda e, g=g, bank=bank: e.scalar_tensor_tensor(
                    out=ya[par][:, g, :], in0=bank[:], scalar=col(C_BPOOL + g), in1=sza[:, g, :],
                    op0=ALU.add, op1=ALU.mult),
                    reads=[bb, b_colt, b_sza[g]], writes=[b_ya[par][g]])


        def CV(k):
            npe = NPE
            for j in range(4):
                bank, bb = nb(True)

                def mm(e, j=j, bank=bank):
                    for t in range(npe):
                        ins = e.matmul(bank[:], lhsT=diag[:, j, t, :],
                                       rhs=vbuf[:, j, HL - (KW - 1) + t:HL - (KW - 1) + t + TT],
                                       start=(t == 0), stop=(t == npe - 1))
                    return ins
                P.op("pe", mm, reads=[b_diag[j], b_vbuf[j]], writes=[bb], name="CV%d.%d" % (k, j), prio=PRIO_CV)
                for t in range(npe, KW):
                    P.op("dve", lambda e, j=j, t=t, bank=bank: e.scalar_tensor_tensor(
                        out=bank[:], in0=vbuf[:, j, HL - (KW - 1) + t:HL - (KW - 1) + t + TT],
                        scalar=col(C_WDW + j * KW + t), in1=bank[:], op0=ALU.mult, op1=ALU.add),
                        reads=[bb, b_vbuf[j], b_colt], writes=[bb])
                P.op("act", lambda e, j=j, bank=bank: e.activation(
                    out=cbf[:, j, :], in_=bank[:], func=AF.Identity, bias=col(C_BDW + j), scale=1.0),
                    reads=[bb, b_colt], writes=[b_cbf[j]])
                P.op("act", lambda e, j=j, bank=bank: e.activation(
                    out=csq[:, j, :], in_=bank[:], func=AF.Square, bias=col(C_BDW + j), scale=1.0),
                    reads=[bb, b_colt], writes=[b_csq[j]])

        def ST(k):
            bM, bbM = nb(True)
            bQ, bbQ = nb(True)

            def mmM(e):
                for j in range(4):
                    ins = e.matmul(bM[:], lhsT=ones512[:], rhs=cbf[:, j, :], start=(j == 0), stop=(j == 3))
                return ins
            P.op("pe", mmM, reads=[b_ones512] + b_cbf, writes=[bbM], name="STm%d" % k, prio=PRIO_ST)

            def mmQ(e):
                for j in range(4):
                    ins = e.matmul(bQ[:], lhsT=ones512[:], rhs=csq[:, j, :], start=(j == 0), stop=(j == 3))
                return ins
            P.op("pe", mmQ, reads=[b_ones512] + b_csq, writes=[bbQ], name="STq%d" % k, prio=PRIO_ST)
            P.op("act", lambda e: e.activation(out=msq[:], in_=bM[:], func=AF.Square),
                 reads=[bbM], writes=[b_msq])
            P.op("dve", lambda e: e.tensor_tensor(out=msq[:], in0=bQ[:], in1=msq[:], op=ALU.subtract),
                 reads=[bbQ, b_msq], writes=[b_msq])
            P.op("act", lambda e: e.activation(out=msq[:], in_=msq[:], func=AF.Sqrt, bias=epst[:], scale=1.0),
                 reads=[b_msq, b_epst], writes=[b_msq])
            P.op("dve", lambda e: e.reciprocal(out=rstd[:], in_=msq[:]), reads=[b_msq], writes=[b_rstd])
            for j in range(4):
                P.op("dve", lambda e, j=j: e.tensor_tensor(out=cbf[:, j, :], in0=cbf[:, j, :], in1=bM[:],
                                                         op=ALU.subtract),
                     reads=[b_cbf[j], bbM], writes=[b_cbf[j]])
                P.op("pool", lambda e, j=j: e.tensor_tensor(out=cbf[:, j, :], in0=cbf[:, j, :], in1=rstd[:],
                                                          op=ALU.mult),
                     reads=[b_cbf[j], b_rstd], writes=[b_cbf[j]])
                P.op("act", lambda e, j=j: e.activation(out=lnv[:, j, :], in_=cbf[:, j, :], func=AF.Silu,
                                                      bias=col(C_LNB + j), scale=col(C_LNG + j)),
                     reads=[b_cbf[j], b_colt], writes=[b_lnv[j]])

        def PW(k):
            for mo in range(4):
                bank, bb = nb()

                def mm(e, mo=mo, bank=bank):
                    for ki in range(4):
                        ins = e.matmul(bank[:], lhsT=wpw[:, ki, mo * 128:(mo + 1) * 128], rhs=lnv[:, ki, :],
                                       start=(ki == 0), stop=(ki == 3))
                    return ins
                P.op("pe", mm, reads=[b_wpw] + b_lnv, writes=[bb], name="PW%d.%d" % (k, mo))
                P.op("dve", lambda e, mo=mo, bank=bank: e.scalar_tensor_tensor(
                    out=yb[:, mo, :], in0=bank[:], scalar=col(C_BPW + mo), in1=szb[:, mo, :],
                    op0=ALU.add, op1=ALU.mult),
                    reads=[bb, b_colt, b_szb[mo]], writes=[b_yb[mo]])

        f_ctr = [0]
        stores = []

        def OPF(k):
            par = k % 2
            for s in range(NSUB):
                r = f_ctr[0] % NF
                f_ctr[0] += 1
                row0 = k * TT + s * 128
                P.op("sp", lambda e, r=r, row0=row0: e.dma_start(out=fbuf[r][:], in_=x[HL + row0:HL + row0 + 128, :]),
                     writes=[b_fbuf[r]], dma=True)
                for h in range(2):
                    bank, bb = nb()

                    def mm(e, h=h, bank=bank, s=s):
                        for kc in range(8):
                            lhs = ya[par][:, kc, s * 128:(s + 1) * 128] if kc < 4 else yb[:, kc - 4, s * 128:(s + 1) * 128]
                            e.matmul(bank[:], lhsT=lhs, rhs=wout[:, kc, h * 512:(h + 1) * 512],
                                     start=(kc == 0), stop=False)
                        return e.matmul(bank[:], lhsT=onesrow[0:1, :], rhs=bprow[0:1, h * 512:(h + 1) * 512],
                                        start=False, stop=True)
                    P.op("pe", mm, reads=b_ya[par] + b_yb + b_wout + [b_onesrow, b_bprow], writes=[bb],
                         name="OP%d.%d.%d" % (k, s, h))

                    P.op("dve", lambda e, h=h, bank=bank, r=r: e.scalar_tensor_tensor(
                        out=fbuf[r][:, h * 512:(h + 1) * 512], in0=fbuf[r][:, h * 512:(h + 1) * 512],
                        scalar=ALPHA, in1=bank[:], op0=ALU.mult, op1=ALU.add),
                        reads=[b_fbuf[r], bb], writes=[b_fbuf[r]])
                    P.op("dve", lambda e, h=h, r=r: e.bn_stats(out=fst[r][:, h, :],
                                                               in_=fbuf[r][:, h * 512:(h + 1) * 512]),
                         reads=[b_fbuf[r]], writes=[b_fst[r]])
                P.op("dve", lambda e, r=r: e.bn_aggr(out=fmv[r][:], in_=fst[r][:].rearrange("p a b -> p (a b)")),
                     reads=[b_fst[r]], writes=[b_fmv[r]])
                ln_rstd(fmv[r], fve[r], frs[r], fnm[r], b_fmv[r], b_fve[r], b_frs[r], b_fnm[r], 128)
                P.op("act", lambda e, r=r: e.activation(out=fbuf[r][:], in_=fbuf[r][:], func=AF.Identity,
                                                      bias=fnm[r][:], scale=frs[r][:]),
                     reads=[b_fbuf[r], b_frs[r], b_fnm[r]], writes=[b_fbuf[r]])
                aff_eng = "dve" if k == NT - 1 else "pool"
                P.op(aff_eng, lambda e, r=r: e.tensor_tensor(out=fbuf[r][:], in0=fbuf[r][:], in1=gpbt[:, 0:D],
                                                           op=ALU.mult),
                     reads=[b_fbuf[r], b_gpbt], writes=[b_fbuf[r]])
                P.op(aff_eng, lambda e, r=r: e.tensor_tensor(out=fbuf[r][:], in0=fbuf[r][:], in1=gpbt[:, D:2 * D],
                                                           op=ALU.add),
                     reads=[b_fbuf[r], b_gpbt], writes=[b_fbuf[r]])
                P.op("sp", lambda e, r=r, row0=row0: e.dma_start(out=y[row0:row0 + 128, :], in_=fbuf[r][:]),
                     reads=[b_fbuf[r]], dma=True)


        MODT()
        for m in range(4):
            bank, bb = ip_group(m, 0, HL, hTh, b_hTh)
            P.op("act", lambda e, m=m, bank=bank: e.activation(
                out=ubuf[:, m, 0:HL], in_=bank[:, 0:HL], func=AF.Identity, bias=bmask[:, m:m + 1],
                scale=col(C_MASK)),
                reads=[bb, b_colt, b_bmask], writes=[b_ubuf[m]])
        for j in range(4):
            m = 12 + j
            bank, bb = ip_group(m, 0, HL, hTh, b_hTh)
            P.op("act", lambda e, m=m, j=j, bank=bank: e.activation(
                out=sig[:, j, 0:HL], in_=bank[:, 0:HL], func=AF.Sigmoid, bias=col(C_BIN + m), scale=1.0),
                reads=[bb, b_colt], writes=[b_sig[j]])
        for j in range(4):
            m = 8 + j
            bank, bb = ip_group(m, 0, HL, hTh, b_hTh)
            P.op("dve", lambda e, m=m, j=j, bank=bank: e.scalar_tensor_tensor(
                out=tmp16[:, 0:HL], in0=bank[:, 0:HL], scalar=col(C_BIN + m), in1=sig[:, j, 0:HL],
                op0=ALU.add, op1=ALU.mult),
                reads=[bb, b_colt, b_sig[j]], writes=[b_tmp16])
            P.op("dve", lambda e, j=j: e.tensor_scalar(out=vbuf[:, j, 0:HL], in0=tmp16[:, 0:HL],
                                                     scalar1=col(C_MASK), scalar2=None, op0=ALU.mult),
                 reads=[b_tmp16, b_colt], writes=[b_vbuf[j]])

        for k in range(NT + 1):
            front = k < NT
            back = k >= 1
            if back:
                CV(k - 1)
            if front:
                if k > 0:
                    TR(k)
                IP_u(k)
                if k + 1 < NT:
                    LN1(k + 1)
                POOLING(k)
            if back:
                ST(k - 1)
            if front:
                IP_za(k)
                IP_glu(k)
            if back:
                PW(k - 1)
            if front:
                GL(k)
                IP_zb(k)
            if back:
                OPF(k - 1)

        if DBG:
            P.op("sp", lambda e: e.dma_start(out=dbg[:, 0:24], in_=modcol[:, 0:24]), reads=[b_modcol], dma=True, stream="dbg0")
            P.op("pool", lambda e: e.dma_start(out=dbg[0:1, 1024:2048], in_=bprow[0:1, :]), reads=[b_bprow], dma=True, stream="dbg1")
            P.op("sp", lambda e: e.dma_start(out=dbg[:, 2048:4096], in_=gpbt[:, :]), reads=[b_gpbt], dma=True, stream="dbg2")
            P.op("sp", lambda e: e.dma_start(out=dbg[:, 32:32 + NCOL], in_=colt[:, :]), reads=[b_colt], dma=True, stream="dbg3")
        P.op("sp", None, reads=b_fbuf, writes=b_fbuf)
        P.emit()
        build_nc.last_prog = P
    return nc


_NC_CACHE = {}


def _pack_inputs(inp, core):
    b, half = core // 2, core % 2
    f32 = np.float32
    x = np.asarray(inp["x"], dtype=f32)
    xe = np.zeros((HL + TOK, D), dtype=f32)
    t0 = half * TOK
    xe[HL:] = x[b, t0:t0 + TOK]
    if half == 1:
        xe[:HL] = x[b, t0 - HL:t0]
    cols = np.zeros((128, NCOL), dtype=f32)
    cols[:, C_C:C_C + 8] = np.asarray(inp["c"], f32)[b].reshape(8, 128).T
    cols[:, C_BIN:C_BIN + 20] = np.asarray(inp["b_in"], f32).reshape(20, 128).T
    cols[:, C_BPOOL:C_BPOOL + 4] = np.asarray(inp["b_pool"], f32).T
    cols[:, C_LS:C_LS + 4] = np.asarray(inp["ls_pool"], f32).reshape(4, 128).T
    wd = np.asarray(inp["w_dw"], f32)[:, 0, :]
    cols[:, C_WDW:C_WDW + 124] = wd.T.reshape(4, 128, KW).transpose(1, 0, 2).reshape(128, 124)
    cols[:, C_BDW:C_BDW + 4] = np.asarray(inp["b_dw"], f32).reshape(4, 128).T
    cols[:, C_LNG:C_LNG + 4] = np.asarray(inp["ln_conv_g"], f32).reshape(4, 128).T
    cols[:, C_LNB:C_LNB + 4] = np.asarray(inp["ln_conv_b"], f32).reshape(4, 128).T
    cols[:, C_BPW:C_BPW + 4] = np.asarray(inp["b_pw"], f32).reshape(4, 128).T
    cols[:, C_MASK] = 1.0 if half == 1 else 0.0
    tg = t0 + np.arange(16)
    for g, w in enumerate(POOLW):
        cols[:, C_INVC + 16 * g:C_INVC + 16 * g + 16] = (1.0 / np.minimum(tg + 1, w)).astype(f32)[None, :]
    rows = np.concatenate([np.asarray(inp["b_ada"], f32), np.asarray(inp["b_out"], f32)]).reshape(4, 1024)
    gpb = np.concatenate([np.broadcast_to(np.asarray(inp["ln_post_g"], f32)[None, :], (128, D)),
                          np.broadcast_to(np.asarray(inp["ln_post_b"], f32)[None, :], (128, D))], axis=1)
    return {
        "x": np.ascontiguousarray(xe),
        "cols": cols,
        "rows": np.ascontiguousarray(rows),
        "gpb": np.ascontiguousarray(gpb),
        "w_ada": np.ascontiguousarray(np.asarray(inp["w_ada"], f32)),
        "w_in": np.ascontiguousarray(np.asarray(inp["w_in"], f32)),
        "w_pool": np.ascontiguousarray(np.asarray(inp["w_pool"], f32)),
        "w_pw": np.ascontiguousarray(np.asarray(inp["w_pw"], f32)),
        "w_out": np.ascontiguousarray(np.asarray(inp["w_out"], f32)),
    }


def kernel(**inputs):
    if "nc" not in _NC_CACHE:
        _NC_CACHE["nc"] = build_nc()
    nc = _NC_CACHE["nc"]
    in_maps = [_pack_inputs(inputs, i) for i in range(NCORES)]
    res = run_bass_kernel_spmd(nc, in_maps, core_ids=list(range(NCORES)))
    out = np.empty((4, SEQ, D), dtype=np.float32)
    for i in range(NCORES):
        b, half = i // 2, i % 2
        out[b, half * TOK:(half + 1) * TOK] = res.results[i]["y"]
    return out
```
